# Optimizing a Trainium2 kernel written in Bass

```python
import jax
import jax.numpy as jnp
from jax import lax
import numpy as np

D_MODEL = 1024
BATCH = 4
SEQ = 8192
DEPTH = 2
DEC_BATCH = 32
DEC_SEQ = 1
PAST_LEN = 16384
PAGE_SIZE = 128

HEAD_DIM = 64
DILATED_GROUPS = ((128, 1), (512, 4), (2048, 16))
N_DIL = len(DILATED_GROUPS)
HEADS_PER_GROUP = 4
ATT_QKV = N_DIL * HEADS_PER_GROUP * HEAD_DIM
ATT_OUT = HEADS_PER_GROUP * HEAD_DIM
CONV_DIM = D_MODEL - ATT_OUT
CONV_WIDTH = 31
D_IN = 3 * ATT_QKV + 2 * CONV_DIM
D_FF = 4 * D_MODEL
ROT_DIM = HEAD_DIM // 4
ROPE_THETA = 500000.0
BLK = 128
ATT_SCALE = HEAD_DIM ** -0.5
RMS_EPS = 1e-6
LN_EPS = 1e-5
NEG_INF = -1e30

kernel_name = "hybrid_dilated_attn_conformer_decoder_step"


def rms_norm(x, g):
    xf = x.astype(jnp.float32)
    y = xf * lax.rsqrt(jnp.mean(xf * xf, axis=-1, keepdims=True) + RMS_EPS)
    return (y * g.astype(jnp.float32)).astype(x.dtype)


def partial_rotary(x, pos):
    half = ROT_DIM // 2
    inv = 1.0 / (ROPE_THETA ** (jnp.arange(0, ROT_DIM, 2, dtype=jnp.float32) / ROT_DIM))
    ang = pos.astype(jnp.float32)[:, None] * inv[None, :]
    bshape = (ang.shape[0],) + (1,) * (x.ndim - 3) + (half,)
    cos = jnp.cos(ang).reshape(bshape)
    sin = jnp.sin(ang).reshape(bshape)
    xr = x[..., :ROT_DIM].astype(jnp.float32)
    x1, x2 = xr[..., :half], xr[..., half:]
    rot = jnp.concatenate([x1 * cos - x2 * sin, x2 * cos + x1 * sin], axis=-1)
    return jnp.concatenate([rot.astype(x.dtype), x[..., ROT_DIM:]], axis=-1)


def dilated_attn_prompt(q, k, v, dil, steps):
    b, s, h, dh = q.shape
    span = dil * BLK
    s_pad = -(-s // span) * span
    length = s_pad // dil
    nb = length // BLK

    def to_sub(t):
        t = jnp.pad(t, ((0, 0), (0, s_pad - s), (0, 0), (0, 0)))
        t = t.reshape(b, length, dil, h, dh).transpose(0, 2, 1, 3, 4)
        return t.reshape(b, dil, nb, BLK, h, dh)

    def with_prev(t):
        prev = jnp.pad(t[:, :, :-1], ((0, 0), (0, 0), (1, 0), (0, 0), (0, 0), (0, 0)))
        return jnp.concatenate([prev, t], axis=3)

    qs = to_sub(q)
    ks = with_prev(to_sub(k))
    vs = with_prev(to_sub(v))
    sc = jnp.einsum('brnqhd,brnkhd->brnhqk', qs, ks, preferred_element_type=jnp.float32) * ATT_SCALE
    qi = jnp.arange(BLK)[:, None]
    kj = jnp.arange(2 * BLK)[None, :] - BLK
    dist = qi - kj
    band = (dist >= 0) & (dist <= steps)
    has_prev = (jnp.arange(nb) > 0)[:, None, None] | (kj >= 0)[None]
    mask = band[None] & has_prev
    sc = jnp.where(mask[None, None, :, None], sc, NEG_INF)
    m = jnp.max(sc, axis=-1, keepdims=True)
    p = jnp.exp(sc - m)
    l = jnp.sum(p, axis=-1, keepdims=True)
    o = jnp.einsum('brnhqk,brnkhd->brnqhd', p, vs.astype(jnp.float32)) / jnp.moveaxis(l, 3, 4)
    lse = jnp.moveaxis((m + jnp.log(l))[..., 0], 3, 4)

    def from_sub(t):
        t = t.reshape(b, dil, length, *t.shape[4:])
        return jnp.moveaxis(t, 1, 2).reshape(b, s_pad, *t.shape[3:])[:, :s]

    return from_sub(o), from_sub(lse)


def dilated_attn_sample(q, kc, vc, n_buf, dil, steps):
    t = q.shape[1]
    idx = n_buf + jnp.arange(t)[:, None] - dil * jnp.arange(steps + 1)[None, :]
    valid = idx >= 0
    idx = jnp.maximum(idx, 0)
    kg = kc[:, idx]
    vg = vc[:, idx]
    sc = jnp.einsum('bthd,btjhd->bhtj', q, kg, preferred_element_type=jnp.float32) * ATT_SCALE
    sc = jnp.where(valid[None, None], sc, NEG_INF)
    m = jnp.max(sc, axis=-1, keepdims=True)
    p = jnp.exp(sc - m)
    l = jnp.sum(p, axis=-1, keepdims=True)
    o = jnp.einsum('bhtj,btjhd->bthd', p, vg.astype(jnp.float32)) / jnp.moveaxis(l, 1, 2)
    lse = jnp.moveaxis((m + jnp.log(l))[..., 0], 1, 2)
    return o, lse


def hybrid_layer(x, pos, kv_bufs, conv_buf, w_in, w_o, conv_w, conv_b, ln_g, ln_b, g_mix, g_ffn, w_up, w_down):
    b, s, _ = x.shape
    n = rms_norm(x, g_mix)
    z = jnp.einsum('bsd,de->bse', n, w_in)
    q, k, v, glu = jnp.split(z, [ATT_QKV, 2 * ATT_QKV, 3 * ATT_QKV], axis=-1)
    shp = (b, s, N_DIL, HEADS_PER_GROUP, HEAD_DIM)
    q = partial_rotary(q.reshape(shp), pos)
    k = partial_rotary(k.reshape(shp), pos)
    v = v.reshape(shp)

    outs, lses, new_kv = [], [], []
    for g, (win, dil) in enumerate(DILATED_GROUPS):
        qg, kg, vg = q[:, :, g], k[:, :, g], v[:, :, g]
        steps = win // dil
        if kv_bufs is None:
            o, lse = dilated_attn_prompt(qg, kg, vg, dil, steps)
            keep = min(win, s)
            new_kv.append(jnp.stack([kg[:, s - keep:], vg[:, s - keep:]], axis=1))
        else:
            buf = kv_bufs[g]
            n_buf = buf.shape[2]
            kc = jnp.concatenate([buf[:, 0].astype(kg.dtype), kg], axis=1)
            vc = jnp.concatenate([buf[:, 1].astype(vg.dtype), vg], axis=1)
            o, lse = dilated_attn_sample(qg, kc, vc, n_buf, dil, steps)
            new_kv.append(jnp.stack([kc[:, -n_buf:], vc[:, -n_buf:]], axis=1))
        outs.append(o)
        lses.append(lse)
    wts = jax.nn.softmax(jnp.stack(lses, axis=0), axis=0)
    att = jnp.sum(wts[..., None] * jnp.stack(outs, axis=0), axis=0)
    att = att.reshape(b, s, ATT_OUT).astype(x.dtype)

    a, gate = jnp.split(glu, 2, axis=-1)
    u = a * jax.nn.sigmoid(gate)
    if conv_buf is None:
        uc = jnp.pad(u, ((0, 0), (CONV_WIDTH - 1, 0), (0, 0)))
    else:
        uc = jnp.concatenate([conv_buf.astype(u.dtype), u], axis=1)
    new_conv = uc[:, -(CONV_WIDTH - 1):]
    c = lax.conv_general_dilated(uc, conv_w[:, None, :].astype(uc.dtype), window_strides=(1,), padding='VALID',
                                 dimension_numbers=('NWC', 'WIO', 'NWC'), feature_group_count=CONV_DIM)
    cf = c.astype(jnp.float32) + conv_b.astype(jnp.float32)
    mu = jnp.mean(cf, axis=-1, keepdims=True)
    var = jnp.mean(jnp.square(cf - mu), axis=-1, keepdims=True)
    cf = (cf - mu) * lax.rsqrt(var + LN_EPS) * ln_g.astype(jnp.float32) + ln_b.astype(jnp.float32)
    cv = (cf * jax.nn.sigmoid(cf)).astype(x.dtype)

    h = x + jnp.einsum('bse,ed->bsd', jnp.concatenate([att, cv], axis=-1), w_o)
    n2 = rms_norm(h, g_ffn)
    f = jnp.square(jax.nn.relu(jnp.einsum('bsd,df->bsf', n2, w_up)))
    y = h + jnp.einsum('bsf,fd->bsd', f, w_down)
    return y, new_kv, new_conv


def setup_inputs(seed: int = 0) -> dict:
    key = jax.random.key(seed)
    ks = jax.random.split(key, 20)
    f32 = jnp.float32

    def nrm(k, shape, scale):
        return jax.random.normal(k, shape, f32) * scale

    def kv_shape(w):
        return (DEPTH, DEC_BATCH, 2, min(w, PAST_LEN), HEADS_PER_GROUP, HEAD_DIM)

    return {
        "x_prompt": nrm(ks[0], (BATCH, SEQ, D_MODEL), 1.0),
        "x_sample": nrm(ks[1], (DEC_BATCH, DEC_SEQ, D_MODEL), 1.0),
        "cache_kv_w128": nrm(ks[2], kv_shape(DILATED_GROUPS[0][0]), 1.0),
        "cache_kv_w512": nrm(ks[3], kv_shape(DILATED_GROUPS[1][0]), 1.0),
        "cache_kv_w2048": nrm(ks[4], kv_shape(DILATED_GROUPS[2][0]), 1.0),
        "state_conv": nrm(ks[5], (DEPTH, DEC_BATCH, CONV_WIDTH - 1, CONV_DIM), 0.5),
        "w_in": nrm(ks[6], (DEPTH, D_MODEL, D_IN), D_MODEL ** -0.5),
        "w_o": nrm(ks[7], (DEPTH, D_MODEL, D_MODEL), D_MODEL ** -0.5),
        "conv_w": nrm(ks[8], (DEPTH, CONV_WIDTH, CONV_DIM), CONV_WIDTH ** -0.5),
        "conv_b": nrm(ks[9], (DEPTH, CONV_DIM), 0.02),
        "conv_ln_g": 1.0 + nrm(ks[10], (DEPTH, CONV_DIM), 0.02),
        "conv_ln_b": nrm(ks[11], (DEPTH, CONV_DIM), 0.02),
        "norm_mix": 1.0 + nrm(ks[12], (DEPTH, D_MODEL), 0.02),
        "norm_ffn": 1.0 + nrm(ks[13], (DEPTH, D_MODEL), 0.02),
        "w_up": nrm(ks[14], (DEPTH, D_MODEL, D_FF), D_MODEL ** -0.5),
        "w_down": nrm(ks[15], (DEPTH, D_FF, D_MODEL), D_FF ** -0.5),
        "norm_final": 1.0 + nrm(ks[16], (D_MODEL,), 0.02),
    }


def reference(x_prompt, x_sample, cache_kv_w128, cache_kv_w512, cache_kv_w2048, state_conv,
              w_in, w_o, conv_w, conv_b, conv_ln_g, conv_ln_b, norm_mix, norm_ffn, w_up, w_down, norm_final):
    pos_p = jnp.arange(x_prompt.shape[1], dtype=jnp.int32)
    pos_s = PAST_LEN + jnp.arange(x_sample.shape[1], dtype=jnp.int32)
    caches = (cache_kv_w128, cache_kv_w512, cache_kv_w2048)
    hp, hs = x_prompt, x_sample
    kv_p = [[] for _ in range(N_DIL)]
    kv_s = [[] for _ in range(N_DIL)]
    conv_p, conv_s = [], []
    for l in range(DEPTH):
        params = (w_in[l], w_o[l], conv_w[l], conv_b[l], conv_ln_g[l], conv_ln_b[l],
                  norm_mix[l], norm_ffn[l], w_up[l], w_down[l])
        hp, nkv_p, ncv_p = hybrid_layer(hp, pos_p, None, None, *params)
        hs, nkv_s, ncv_s = hybrid_layer(hs, pos_s, [c[l] for c in caches], state_conv[l], *params)
        for g in range(N_DIL):
            kv_p[g].append(nkv_p[g])
            kv_s[g].append(nkv_s[g])
        conv_p.append(ncv_p)
        conv_s.append(ncv_s)
    y_prompt = rms_norm(hp, norm_final)
    y_sample = rms_norm(hs, norm_final)
    new_kv_w128_prompt = jnp.stack(kv_p[0])
    new_kv_w512_prompt = jnp.stack(kv_p[1])
    new_kv_w2048_prompt = jnp.stack(kv_p[2])
    new_conv_prompt = jnp.stack(conv_p)
    new_kv_w128_sample = jnp.stack(kv_s[0])
    new_kv_w512_sample = jnp.stack(kv_s[1])
    new_kv_w2048_sample = jnp.stack(kv_s[2])
    new_conv_sample = jnp.stack(conv_s)
    return (y_prompt, y_sample, new_kv_w128_prompt, new_kv_w512_prompt, new_kv_w2048_prompt, new_conv_prompt,
            new_kv_w128_sample, new_kv_w512_sample, new_kv_w2048_sample, new_conv_sample)
```

```python
from contextlib import ExitStack
import numpy as np
import concourse.bass as bass
import concourse.mybir as mybir
from concourse.bass_utils import run_bass_kernel_spmd

F32 = mybir.dt.float32
BF16 = mybir.dt.bfloat16
AF = mybir.ActivationFunctionType
ALU = mybir.AluOpType

D = 1024
NW = 8192
NT = NW + 128
DQ = 768
DIN = 3840
DFF = 4096
DILS = (1, 4, 16)
NBUF = (128, 512, 2048)
RMS_EPS = 1e-6
_SKIP_SHIFT = False
_CUT = 99
_PF = 2
_HSEL = (0, 1, 2, 3)
_BSEL = (0, 1)
_NOSAMP = False
_STOP_AFTER = None
LN_EPS = 1e-5


class Sch:
    def __init__(self, nc):
        self.nc = nc
        self.eng = {'pe': nc.tensor, 'act': nc.scalar, 'dve': nc.vector, 'pool': nc.gpsimd, 'sp': nc.sync}
        self.sem = {e: nc.alloc_semaphore(name="s_" + e) for e in ['pe', 'act', 'dve', 'pool']}
        self.cnt = {e: 0 for e in self.sem}
        self.known = {e: {} for e in self.eng}
        self.res = {}
        self.dsem = {}
        self.log = []

    def _wait(self, e, toks):
        best = {}
        for (k, h, v) in toks:
            if k == e and e in self.cnt:
                if e == 'pe':
                    continue
                if v <= self.cnt[e] - 3:
                    continue
            if self.known[e].get(k, 0) >= v:
                continue
            if k not in best or best[k][1] < v:
                best[k] = (h, v)
        for k, (h, v) in best.items():
            self.eng[e].wait_ge(h, v)
            self.known[e][k] = v
            self.log.append(('wait', e, k, v))

    def _deps(self, reads, writes, nowaw=False):
        toks = []
        for r in reads:
            st = self.res.get(r)
            if st:
                toks.extend(st['w'])
        for w in writes:
            st = self.res.get(w)
            if st:
                if not nowaw:
                    toks.extend(st['w'])
                else:
                    toks.extend(st.get('pr', []))
                toks.extend(st['r'])
        return toks

    def _commit(self, tok, reads, writes, nowaw=False):
        for r in reads:
            st = self.res.setdefault(r, {'w': [], 'r': []})
            st['r'] = [t for t in st['r'] if t[0] != tok[0]] + [tok]
        for w in writes:
            if nowaw and w in self.res:
                st = self.res[w]
                st['w'] = [t for t in st['w'] if t[0] != tok[0]] + [tok]
            else:
                old = self.res.get(w)
                self.res[w] = {'w': [tok], 'r': [], 'pr': (old['r'] if old else [])}

    def op(self, e, reads, writes, fn, nowaw=False):
        self.group(e, reads, writes, [fn], nowaw)

    def group(self, e, reads, writes, fns, nowaw=False):
        self._wait(e, self._deps(reads, writes, nowaw))
        ins = None
        for fn in fns:
            ins = fn(self.eng[e])
        self.cnt[e] += 1
        ins.then_inc(self.sem[e], 1)
        self._commit((e, self.sem[e], self.cnt[e]), reads, writes, nowaw)
        self.log.append(('op', e, self.cnt[e], tuple(reads), tuple(writes)))

    def dma(self, q, semkey, reads, writes, fn, nowaw=False):
        self._wait(q, self._deps(reads, writes, nowaw))
        if semkey not in self.dsem:
            self.dsem[semkey] = [self.nc.alloc_semaphore(name="d_" + str(semkey)), 0]
        ds = self.dsem[semkey]
        ins = fn(self.eng[q])
        ds[1] += 16
        ins.then_inc(ds[0], 16)
        self._commit((('d', semkey), ds[0], ds[1]), reads, writes, nowaw)
        self.log.append(('dma', q, semkey, ds[1], tuple(reads), tuple(writes)))

    def barrier(self, final=False):
        toks = [(e, self.sem[e], self.cnt[e]) for e in self.sem if self.cnt[e] > 0]
        toks += [(('d', k), v[0], v[1]) for k, v in self.dsem.items() if v[1] > 0 and (final or k != 'shift')]
        for e in self.eng:
            for (k, h, v) in toks:
                if k == e:
                    continue
                if self.known[e].get(k, 0) >= v:
                    continue
                self.eng[e].wait_ge(h, v)
                self.known[e][k] = v
        self.res = {}


def bcast_rows(ap2d, n):
    a = ap2d.ap
    return bass.AP(ap2d.tensor, ap2d.offset, [[0, n], [a[-1][0], a[-1][1]]])


def build_program(dbg=None):
    nc = bass.Bass("TRN2", target_bir_lowering=False)
    din = lambda n, sh: nc.dram_tensor(n, sh, F32, kind="ExternalInput").ap()
    dout = lambda n, sh: nc.dram_tensor(n, sh, F32, kind="ExternalOutput").ap()
    dscr = lambda n, sh, dt: nc.dram_tensor(n, sh, dt).ap()

    xw = din("xw", [NT, D] if dbg is None else [128, D])
    validT = din("validT", [128, NT // 128])
    cosT = din("cosT", [128, NT // 128, 8])
    sinT = din("sinT", [128, NT // 128, 8])
    ident_d = din("ident", [128, 128])
    mask_d = din("mask", [128, 256])
    w_in = din("w_in", [2, D, DIN] if dbg is None else [2, 128, 128])
    w_o = din("w_o", [2, D, D] if dbg is None else [2, 128, 128])
    w_up = din("w_up", [2, D, DFF] if dbg is None else [2, 128, 128])
    w_down = din("w_down", [2, DFF, D] if dbg is None else [2, 128, 128])
    convwT = din("convwT", [2, 128, 6, 31])
    convp = din("convp", [2, 128, 3, 6])
    gmix = din("gmix", [2, 128, 8])
    gffn = din("gffn", [2, 128, 8])
    gfin = din("gfin", [1, D])
    ckv = [din("ckv%d" % g, [2, 4, 2, NBUF[g], 256]) for g in range(3)]
    sconv = din("sconv", [2, 4, 30, DQ])

    y_o = dout("y", [4096 + 128, D])
    kout = dout("kout", [2, 2048, DQ])
    vout = dout("vout", [2, 2048, DQ])
    convp_o = dout("convp_o", [2, DQ, 30])
    okv = [dout("okv%d" % g, [2, 4, 2, NBUF[g], 256]) for g in range(3)]
    convs_o = dout("convs_o", [2, 4, 29, DQ])
    convs_new = dout("convs_new", [2, DQ, 4])

    if dbg == 'B1':
        qs = nc.dram_tensor("qs", [NT, DQ], BF16, kind="ExternalInput").ap()
        ks = nc.dram_tensor("ks", [NT, DQ], BF16, kind="ExternalInput").ap()
        vs = nc.dram_tensor("vs", [NT, 780], BF16, kind="ExternalInput").ap()
    else:
        qs = dscr("qs", [NT, DQ], BF16)
        ks = dscr("ks", [NT, DQ], BF16)
        vs = dscr("vs", [NT, 780], BF16)
    us_T = dscr("us_T", [DQ, NT] if dbg is None else [128, 128], BF16)
    if dbg == 'B1':
        ac_T = nc.dram_tensor("ac_T", [D, NT], BF16, kind="ExternalOutput").ap()
    else:
        ac_T = dscr("ac_T", [D, NT], BF16)
    hs = dscr("hs", [NT, D] if dbg is None else [128, 128], F32)
    n2_T = dscr("n2_T", [D, NT] if dbg is None else [128, 128], BF16)
    f_T = dscr("f_T", [DFF, NT] if dbg is None else [128, 128], BF16)
    x1 = dscr("x1", [NT, D] if dbg is None else [128, 128], F32)

    s = Sch(nc)
    nc._sch = s
    uid = [0]

    def un(n):
        uid[0] += 1
        return "%s_%d" % (n, uid[0])

    def tiles_for(lo):
        t = [(c0, 4) for c0 in range(lo, NW, 512)]
        t.append((NW, 1))
        return t

    with ExitStack() as top:
        sbt = lambda n, sh, dt: top.enter_context(nc.sbuf_tensor(n, sh, dt))
        ident_b = sbt("ident_b", [128, 128], BF16)
        ident_f = sbt("ident_f", [128, 128], F32)
        mask_b = sbt("mask_b", [128, 256], BF16)
        ones_b = sbt("ones_b", [128, 128], BF16)
        ones_f = sbt("ones_f", [128, 64], F32)
        valid_sb = sbt("valid_sb", [128, NT // 128], F32)
        cos_sb = sbt("cos_sb", [128, NT // 128, 8], F32)
        sin_sb = sbt("sin_sb", [128, NT // 128, 8], F32)
        gmix_sb = sbt("gmix_sb", [128, 2, 8], F32)
        gffn_sb = sbt("gffn_sb", [128, 2, 8], F32)
        gfin_sb = sbt("gfin_sb", [128, D], F32)

        s.dma('pool', 'c0', [], ['ident_b'], lambda q: q.dma_start(out=ident_b[:], in_=ident_d))
        s.dma('sp', 'c1', [], ['ident_f'], lambda q: q.dma_start(out=ident_f[:], in_=ident_d))
        s.dma('pool', 'c0', [], ['mask_b'], lambda q: q.dma_start(out=mask_b[:], in_=mask_d))
        s.dma('sp', 'c1', [], ['valid'], lambda q: q.dma_start(out=valid_sb[:], in_=validT))
        s.dma('sp', 'c1', [], ['cos'], lambda q: q.dma_start(out=cos_sb[:], in_=cosT))
        s.dma('sp', 'c1', [], ['sin'], lambda q: q.dma_start(out=sin_sb[:], in_=sinT))
        for l in range(2):
            s.dma('sp', 'c1', [], ['gmix'], lambda q, l=l: q.dma_start(out=gmix_sb[:, l, :], in_=gmix[l]))
            s.dma('sp', 'c1', [], ['gffn'], lambda q, l=l: q.dma_start(out=gffn_sb[:, l, :], in_=gffn[l]))
        s.dma('sp', 'c1', [], ['gfin'], lambda q: q.dma_start(out=gfin_sb[:], in_=bcast_rows(gfin, 128)))
        s.op('dve', [], ['ones_b'], lambda e: e.memset(ones_b[:], 1.0))
        s.op('dve', [], ['ones_f'], lambda e: e.memset(ones_f[:], 1.0))
        shift_q = []

        def emit_shift_some(nmax):
            for _ in range(min(nmax, len(shift_q))):
                shift_q.pop(0)()

        def emit_shift_copies():
            for g in range(3):
                n = NBUF[g]
                for l in range(2):
                    for b4 in range(4):
                        for kv in range(2):
                            if _SKIP_SHIFT:
                                continue
                            m = n - 16
                            shift_q.append(lambda g=g, l=l, m=m, b4=b4, kv=kv: s.dma('sp', 'shift', [], [], lambda q: q.dma_start(
                                out=okv[g][l, b4, kv, 0:m, :].rearrange("(a r) c -> a r c", a=16),
                                in_=ckv[g][l, b4, kv, 1:m + 1, :].rearrange("(a r) c -> a r c", a=16))))
                            shift_q.append(lambda g=g, l=l, m=m, n=n, b4=b4, kv=kv: s.dma('sp', 'shift', [], [], lambda q: q.dma_start(
                                out=okv[g][l, b4, kv, m:n - 1, :], in_=ckv[g][l, b4, kv, m + 1:n, :])))
            for l in range(2):
                shift_q.append(lambda l=l: s.dma('sp', 'shift', [], [], lambda q: q.dma_start(out=convs_o[l], in_=sconv[l, :, 1:30, :])))
        emit_shift_copies()
        s.barrier()

        if _STOP_AFTER == 'pre':
            s.barrier(final=True)
            return nc
        def norm_transpose(st, ph, xt, nb, g_sb_col, nT, tag, ps_pool):
            norm_part(st, xt, nb)
            trans_part(st, nb, g_sb_col, nT, ps_pool)

        def norm_part(st, xt, nb, part=None):
            ss, rstd, junk, xn = st['ss'], st['rstd'], st['junk'], st['xn']
            if part is not None:
                if part == 0:
                    for b in range(nb):
                        s.op('act', [xt], ['junk', 'ss'], lambda e, b=b: e.activation(
                            out=junk[:], in_=xt_ap(xt)[:, b, :], func=AF.Square, accum_out=ss[:, b:b + 1]))
                elif part == 1:
                    s.op('dve', ['ss'], ['rstd'], lambda e: e.tensor_scalar(
                        out=rstd[:, 0:nb], in0=ss[:, 0:nb], scalar1=1.0 / D, scalar2=RMS_EPS, op0=ALU.mult, op1=ALU.add))
                    s.op('act', ['rstd'], ['rstd'], lambda e: e.activation(out=rstd[:, 0:nb], in_=rstd[:, 0:nb], func=AF.Sqrt))
                    s.op('dve', ['rstd'], ['rstd'], lambda e: e.reciprocal(out=rstd[:, 0:nb], in_=rstd[:, 0:nb]))
                else:
                    for b in range(nb):
                        s.op('act', [xt, 'rstd'], ['xn'], lambda e, b=b: e.activation(
                            out=xn[:, b, :], in_=xt_ap(xt)[:, b, :], func=AF.Copy, scale=rstd[:, b:b + 1]))
                return
            for b in range(nb):
                s.op('act', [xt], ['junk', 'ss'], lambda e, b=b: e.activation(
                    out=junk[:], in_=xt_ap(xt)[:, b, :], func=AF.Square, accum_out=ss[:, b:b + 1]))
            s.op('dve', ['ss'], ['rstd'], lambda e: e.tensor_scalar(
                out=rstd[:, 0:nb], in0=ss[:, 0:nb], scalar1=1.0 / D, scalar2=RMS_EPS, op0=ALU.mult, op1=ALU.add))
            s.op('act', ['rstd'], ['rstd'], lambda e: e.activation(out=rstd[:, 0:nb], in_=rstd[:, 0:nb], func=AF.Sqrt))
            s.op('dve', ['rstd'], ['rstd'], lambda e: e.reciprocal(out=rstd[:, 0:nb], in_=rstd[:, 0:nb]))
            for b in range(nb):
                s.op('act', [xt, 'rstd'], ['xn'], lambda e, b=b: e.activation(
                    out=xn[:, b, :], in_=xt_ap(xt)[:, b, :], func=AF.Copy, scale=rstd[:, b:b + 1]))

        def trans_part(st, nb, g_sb_col, nT, ps_pool):
            xn = st['xn']
            for k in range(8):
                pT = ps_pool[k % 2]
                s.group('pe', ['xn', 'ident_b'], [pT], [
                    (lambda e, b=b, k=k, pT=pT: e.transpose(out=ps_ap(pT)[:, b, :], in_=xn[:, b, k * 128:(k + 1) * 128],
                                                            identity=ident_b[:])) for b in range(nb)])
                s.op('dve', [pT, g_sb_col[0]], [nT], lambda e, k=k, pT=pT: e.tensor_scalar(
                    out=nt_ap(nT)[:, k, 0:nb * 128].rearrange("p (b n) -> p b n", n=128), in0=ps_ap(pT)[:, 0:nb, :], scalar1=g_sb_col[1][:, k:k + 1],
                    scalar2=None, op0=ALU.mult))

        bufs = {}

        def xt_ap(name):
            return bufs[name]

        def ps_ap(name):
            return bufs[name]

        def nt_ap(name):
            return bufs[name]

        def load_weight(st, name, src, kchunks, ncols, q='pool'):
            wsb = st.enter_context(nc.sbuf_tensor(un(name), [128, kchunks, ncols], BF16))
            for k in range(kchunks):
                s.dma(q, 'w' + name, [], [name], lambda qq, k=k: qq.dma_start(out=wsb[:, k, :], in_=src[k * 128:(k + 1) * 128, :]), nowaw=(k > 0))
            return wsb

        def load_weight_cols(st, name, src, kchunks, ncols, blk, order=None):
            wsb = st.enter_context(nc.sbuf_tensor(un(name), [128, kchunks, ncols], BF16))
            nblk = ncols // blk
            for cb in (order if order is not None else range(nblk)):
                s.dma('pool', 'w%s%d' % (name, cb), [], ['%s%d' % (name, cb)], lambda qq, cb=cb: qq.dma_start(
                    out=wsb[:, :, cb * blk:(cb + 1) * blk],
                    in_=src[:, cb * blk:(cb + 1) * blk].rearrange("(k p) n -> p k n", p=128)))
            return wsb

        for l in range(2):
            lo = 0 if l == 0 else 2048
            flo = lo + 2048
            xin = xw if l == 0 else x1

            with ExitStack() as st:
              if dbg is None:
                  sb = lambda n, sh, dt: st.enter_context(nc.sbuf_tensor(un(n), sh, dt))
                  ps = lambda n, sh, dt: st.enter_context(nc.psum_tensor(un(n), sh, dt))
                  win = load_weight_cols(st, "win", w_in[l], 8, DIN, 768, order=[1, 2, 0, 3, 4])
                  stn = {'ss': sb("ss", [128, 4], F32), 'rstd': sb("rstd", [128, 4], F32),
                         'junk': sb("junk", [128, D], F32), 'xn': sb("xn", [128, 4, D], BF16)}
                  for i in range(2):
                      bufs['xtA%d' % i] = sb("xtA%d" % i, [128, 4, D], F32)
                      bufs['nTA%d' % i] = sb("nTA%d" % i, [128, 8, 512], BF16)
                      bufs['pTA%d' % i] = ps("pTA%d" % i, [128, 4, 128], BF16)
                      bufs['pz%d' % i] = ps("pz%d" % i, [128, 512], F32)
                      bufs['qkb%d' % i] = sb("qkb%d" % i, [128, 1536], BF16)
                      bufs['vaug%d' % i] = sb("vaug%d" % i, [128, 12, 65], BF16)
                  for i in range(4):
                      bufs['pg%d' % i] = ps("pg%d" % i, [128, 512], F32)
                  bufs['zt0'] = sb("zt0", [128, 2304], F32)
                  bufs['zt1'] = sb("zt1", [128, 2304], F32)
                  bufs['uT0'] = sb("uT0", [128, 6, 512], F32)
                  bufs['ub0'] = sb("ub0", [128, 6, 512], BF16)
                  sig = [sb("sig%d" % i, [128, 512], F32) for i in range(2)]
                  rt = [sb("rt%d" % i, [128, 24, 8], F32) for i in range(4)]
                  ones12 = sb("ones12", [128, 12, 1], F32)
                  s.op('dve', [], ['ones12'], lambda e: e.memset(ones12[:], 1.0))

                  tlA = tiles_for(lo)

                  def loadA(ti):
                      c0, nb = tlA[ti]
                      xt = 'xtA%d' % (ti % 2)
                      s.dma('sp', xt, [], [xt], lambda q: q.dma_start(
                          out=bufs[xt][:, 0:nb, :], in_=xin[c0:c0 + nb * 128, :].rearrange("(b p) d -> p b d", p=128)))
                  loadA(0)
                  norm_part(stn, 'xtA0', tlA[0][1])
                  trans_part(stn, tlA[0][1], ('gmix', gmix_sb[:, l, :]), 'nTA0', ['pTA0', 'pTA1'])
                  for ti, (c0, nb) in enumerate(tlA):
                      if ti + 1 < len(tlA):
                          loadA(ti + 1)
                      N = nb * 128
                      xt = 'xtA%d' % (ti % 2)
                      nT = 'nTA%d' % (ti % 2)
                      nTt = bufs[nT]
                      for b in range(nb):
                          if b >= 1 and ti + 1 < len(tlA):
                              norm_part(stn, 'xtA%d' % ((ti + 1) % 2), tlA[ti + 1][1], part=b - 1)
                          blk = c0 // 128 + b
                          bi = (ti * 4 + b) % 2
                          zt, qkb, vaug = 'zt%d' % bi, 'qkb%d' % bi, 'vaug%d' % bi
                          ztt, qkbt, vaugt = bufs[zt], bufs[qkb], bufs[vaug]
                          pieces = [(0, 512), (512, 1024), (1024, 1536), (1536, 2048), (2048, 2304)]
                          if c0 + 512 <= flo:
                              pieces = [(768, 1280), (1280, 1792), (1792, 2304)]
                          for pi, (a0, a1) in enumerate(pieces):
                              pz = 'pz%d' % (pi % 2)
                              s.group('pe', [nT] + ['win%d' % cb for cb in range(a0 // 768, (a1 - 1) // 768 + 1)], [pz], [
                                  (lambda e, k=k, pz=pz, a0=a0, a1=a1, b=b: e.matmul(
                                      out=bufs[pz][:, 0:a1 - a0], lhsT=nTt[:, k, b * 128:(b + 1) * 128],
                                      rhs=win[:, k, a0:a1], start=(k == 0), stop=(k == 7))) for k in range(8)])
                              s.op('act', [pz], [zt], lambda e, pz=pz, a0=a0, a1=a1, ztt=ztt: e.activation(
                                  out=ztt[:, a0:a1], in_=bufs[pz][:, 0:a1 - a0], func=AF.Copy))
                          zv = ztt[:, 0:1536].rearrange("p (h d) -> p h d", d=64)
                          x1v, x2v = zv[:, :, 0:8], zv[:, :, 8:16]
                          cb = cos_sb[:, blk, :]
                          sn = sin_sb[:, blk, :]
                          cbb = bass.AP(cb.tensor, cb.offset, [list(cb.ap[0]), [0, 24], list(cb.ap[-1])])
                          snb = bass.AP(sn.tensor, sn.offset, [list(sn.ap[0]), [0, 24], list(sn.ap[-1])])
                          s.op('dve', [zt], ['rt0'], lambda e, x1v=x1v, cbb=cbb: e.tensor_tensor(out=rt[0][:], in0=x1v, in1=cbb, op=ALU.mult))
                          s.op('dve', [zt], ['rt1'], lambda e, x2v=x2v, snb=snb: e.tensor_tensor(out=rt[1][:], in0=x2v, in1=snb, op=ALU.mult))
                          s.op('dve', [zt], ['rt2'], lambda e, x2v=x2v, cbb=cbb: e.tensor_tensor(out=rt[2][:], in0=x2v, in1=cbb, op=ALU.mult))
                          s.op('dve', [zt], ['rt3'], lambda e, x1v=x1v, snb=snb: e.tensor_tensor(out=rt[3][:], in0=x1v, in1=snb, op=ALU.mult))
                          s.op('dve', ['rt0', 'rt1'], [zt], lambda e, x1v=x1v: e.tensor_tensor(out=x1v, in0=rt[0][:], in1=rt[1][:], op=ALU.subtract))
                          s.op('dve', ['rt2', 'rt3'], [zt], lambda e, x2v=x2v: e.tensor_tensor(out=x2v, in0=rt[2][:], in1=rt[3][:], op=ALU.add))
                          s.op('pool', [zt], [qkb], lambda e, ztt=ztt, qkbt=qkbt: e.tensor_copy(out=qkbt[:], in_=ztt[:, 0:1536]))
                          s.op('pool', [zt], [vaug], lambda e, ztt=ztt, vaugt=vaugt: e.tensor_copy(
                              out=vaugt[:, :, 0:64], in_=ztt[:, 1536:2304].rearrange("p (h d) -> p h d", d=64)))
                          s.op('pool', ['ones12'], [vaug], lambda e, vaugt=vaugt, blk=blk: e.tensor_scalar(
                              out=vaugt[:, :, 64:65], in0=ones12[:], scalar1=valid_sb[:, blk:blk + 1], scalar2=None, op0=ALU.mult))
                          r0 = c0 + b * 128
                          s.dma('sp', qkb, [qkb], [], lambda q, qkbt=qkbt, r0=r0: q.dma_start(out=qs[r0:r0 + 128, :], in_=qkbt[:, 0:768]))
                          s.dma('sp', qkb, [qkb], [], lambda q, qkbt=qkbt, r0=r0: q.dma_start(out=ks[r0:r0 + 128, :], in_=qkbt[:, 768:1536]))
                          s.dma('sp', vaug, [vaug], [], lambda q, vaugt=vaugt, r0=r0: q.dma_start(
                              out=vs[r0:r0 + 128, :], in_=vaugt[:].rearrange("p h d -> p (h d)")))
                          if 6144 <= r0 < 8192:
                              s.dma('sp', zt, [zt], [], lambda q, ztt=ztt, r0=r0: q.dma_start(out=kout[l, r0 - 6144:r0 - 6144 + 128, :], in_=ztt[:, 768:1536]))
                              s.dma('sp', zt, [zt], [], lambda q, ztt=ztt, r0=r0: q.dma_start(out=vout[l, r0 - 6144:r0 - 6144 + 128, :], in_=ztt[:, 1536:2304]))
                          if r0 == NW:
                              for g in range(3):
                                  n = NBUF[g]
                                  for kv in range(2):
                                      cs0 = 768 + 768 * kv + 256 * g
                                      s.dma('sp', zt, [zt], [], lambda q, ztt=ztt, g=g, kv=kv, n=n, cs0=cs0: q.dma_start(
                                          out=okv[g][l, :, kv, n - 1, :], in_=ztt[0:4, cs0:cs0 + 256]))
                      if c0 + 512 <= flo - 512:
                          if ti + 1 < len(tlA):
                              trans_part(stn, tlA[ti + 1][1], ('gmix', gmix_sb[:, l, :]), 'nTA%d' % ((ti + 1) % 2), ['pTA0', 'pTA1'])
                          continue
                      ui = 0
                      uT, ub = 'uT%d' % ui, 'ub%d' % ui
                      uTt, ubt = bufs[uT], bufs[ub]
                      for cc in range(6):
                          pa, pgn = 'pg%d' % ((cc % 2) * 2), 'pg%d' % ((cc % 2) * 2 + 1)
                          sg = sig[cc % 2]
                          for (pn, coff) in ((pa, 2304 + cc * 128), (pgn, 2304 + 768 + cc * 128)):
                              s.group('pe', [nT, 'win%d' % (coff // 768)], [pn], [
                                  (lambda e, k=k, pn=pn, coff=coff: e.matmul(
                                      out=bufs[pn][:, 0:N], lhsT=win[:, k, coff:coff + 128], rhs=nTt[:, k, 0:N],
                                      start=(k == 0), stop=(k == 7))) for k in range(8)])
                          s.op('act', [pgn], ['sig%d' % (cc % 2)], lambda e, pgn=pgn, sg=sg: e.activation(
                              out=sg[:, 0:N], in_=bufs[pgn][:, 0:N], func=AF.Sigmoid))
                          s.op('dve', [pa, 'sig%d' % (cc % 2)], [uT], lambda e, pa=pa, sg=sg, cc=cc, uTt=uTt: e.tensor_tensor(
                              out=uTt[:, cc, 0:N], in0=bufs[pa][:, 0:N], in1=sg[:, 0:N], op=ALU.mult))
                      s.op('pool', [uT], [ub], lambda e, uTt=uTt, ubt=ubt: e.tensor_copy(out=ubt[:, :, 0:N], in_=uTt[:, :, 0:N]))
                      s.dma('sp', ub, [ub], [], lambda q, ubt=ubt, c0=c0: q.dma_start(
                          out=us_T[:, c0:c0 + N].rearrange("(cc p) n -> p cc n", p=128), in_=ubt[:, :, 0:N]))
                      if c0 == NW - 512:
                          s.dma('sp', uT, [uT], [], lambda q, uTt=uTt: q.dma_start(
                              out=convp_o[l].rearrange("(cc p) j -> p cc j", p=128), in_=uTt[:, :, 482:512]))
                      if c0 == NW:
                          s.dma('sp', uT, [uT], [], lambda q, uTt=uTt: q.dma_start(
                              out=convs_new[l].rearrange("(cc p) j -> p cc j", p=128), in_=uTt[:, :, 0:4]))
                      if ti + 1 < len(tlA):
                          trans_part(stn, tlA[ti + 1][1], ('gmix', gmix_sb[:, l, :]), 'nTA%d' % ((ti + 1) % 2), ['pTA0', 'pTA1'])
                  s.barrier()
            if _STOP_AFTER == (l, 'A'):
                s.barrier(final=True)
                return nc

            with ExitStack() as st:
                sb = lambda n, sh, dt: st.enter_context(nc.sbuf_tensor(un(n), sh, dt))
                ps = lambda n, sh, dt: st.enter_context(nc.psum_tensor(un(n), sh, dt))
                acc = sb("acc", [128, 4, 2048], F32)
                accS = sb("accS", [128, 4, 4], F32)
                ab = sb("ab", [64, 4, 512], BF16)
                rr = sb("rr", [128, 512], F32)
                for i in range(2):
                    bufs['Qt%d' % i] = sb("Qt%d" % i, [128, 256], BF16)
                    bufs['Kt%d' % i] = sb("Kt%d" % i, [128, 2, 256], BF16)
                    bufs['Vt%d' % i] = sb("Vt%d" % i, [128, 2, 4, 65], BF16)
                    bufs['TT%d' % i] = sb("TT%d" % i, [128, 8, 128], BF16)
                    bufs['Pt%d' % i] = sb("Pt%d" % i, [128, 1024], BF16)
                    bufs["pTT%d" % i] = ps("pTT%d" % i, [128, 8, 128], BF16)
                    bufs['pS%d' % i] = ps("pS%d" % i, [128, 1024], F32)
                    bufs['pO%d' % i] = ps("pO%d" % i, [128, 4, 128], F32)
                for i in range(2):
                    s.op('dve', [], ['Vt%d' % i], lambda e, i=i: e.memset(bufs['Vt%d' % i][:], 1.0))
                    s.op('dve', [], ['TT%d' % i], lambda e, i=i: e.memset(bufs['TT%d' % i][:], 0.0))
                sel_f = sb("sel_f", [128, 64], F32)
                s.op('dve', [], ['sel_f'], lambda e: e.memset(sel_f[:], 0.0))
                s.op('dve', ['sel_f'], ['sel_f'], lambda e: e.memset(sel_f[64:65, :], 1.0))
                s.op('dve', [], ['accS'], lambda e: e.memset(accS[:], 0.0))
                for i in range(2, 4):
                    bufs['Qt%d' % i] = sb("Qt%d" % i, [128, 256], BF16)
                    bufs['Kt%d' % i] = sb("Kt%d" % i, [128, 2, 256], BF16)
                    bufs['Vt%d' % i] = sb("Vt%d" % i, [128, 2, 4, 65], BF16)
                for i in range(4):
                    bufs['VtS%d' % i] = sb("VtS%d" % i, [128, 2, 4, 65], BF16)
                    s.op('dve', [], ['VtS%d' % i], lambda e, i=i: e.memset(bufs['VtS%d' % i][:], 1.0))
                units = []

                def unit_load(idx):
                    u = units[idx]
                    g = u['g']
                    Qt, Kt = 'Qt%d' % (idx % 4), 'Kt%d' % (idx % 4)
                    Vt = ('VtS%d' % (idx % 4)) if u['samp'] else ('Vt%d' % (idx % 4))
                    Qtt, Ktt, Vtt = bufs[Qt], bufs[Kt], bufs[Vt]
                    s.dma('sp', Qt, [], [Qt], lambda q: q.dma_start(out=Qtt[:], in_=u['q']))
                    if not u['samp']:
                        s.dma('sp', Kt, [], [Kt], lambda q: q.dma_start(
                            out=Ktt[:], in_=u['k'].rearrange("(two i) c -> i two c", two=2)))
                        s.dma('sp', Vt, [], [Vt], lambda q: q.dma_start(
                            out=Vtt[:], in_=u['v'].rearrange("(two i) (h d) -> i two h d", two=2, d=65)))
                    else:
                        s.dma('pool', Kt + 'p', [], [Kt], lambda q: q.dma_start(out=Ktt[:, 0, :], in_=u['kp']))
                        s.dma('sp', Kt, [], [Kt], lambda q: q.dma_start(out=Ktt[:, 1, :], in_=u['k']), nowaw=True)
                        s.dma('pool', Vt + 'p', [], [Vt], lambda q: q.dma_start(
                            out=Vtt[:, 0, :, 0:64], in_=u['vp'].rearrange("p (h d) -> p h d", d=64)))
                        s.dma('sp', Vt, [], [Vt], lambda q: q.dma_start(
                            out=Vtt[:, 1, :, :], in_=u['v'].rearrange("p (h d) -> p h d", d=65)), nowaw=True)

                def unit_compute(idx):
                    u = units[idx]
                    i = idx % 2
                    Qt, Kt = 'Qt%d' % (idx % 4), 'Kt%d' % (idx % 4)
                    Vt = ('VtS%d' % (idx % 4)) if u['samp'] else ('Vt%d' % (idx % 4))
                    TT, Pt = 'TT%d' % i, 'Pt%d' % i
                    pTT, pS, pO = 'pTT%d' % i, 'pS%d' % i, 'pO%d' % i
                    Qtt, Ktt, Vtt, TTt, Ptt = bufs[Qt], bufs[Kt], bufs[Vt], bufs[TT], bufs[Pt]
                    pTTt, pSt, pOt = bufs[pTT], bufs[pS], bufs[pO]
                    Ph = [Pt + 'h%d' % h for h in range(4)]
                    fns = []
                    for pr in range(2):
                        fns.append(lambda e, pr=pr: e.transpose(out=pTTt[:, pr, :], in_=Qtt[:, pr * 128:(pr + 1) * 128], identity=ident_b[:]))
                        for bk in range(2):
                            fns.append(lambda e, pr=pr, bk=bk: e.transpose(out=pTTt[:, 2 + pr * 2 + bk, :],
                                                                           in_=Ktt[:, bk, pr * 128:(pr + 1) * 128], identity=ident_b[:]))
                    s.group('pe', [Qt, Kt, 'ident_b'], [pTT], fns)
                    s.op('dve', [pTT], [TT], lambda e: e.tensor_copy(out=TTt[:, 0:4, :], in_=pTTt[:, 2:6, :]))
                    s.op('dve', [pTT], [TT], lambda e: e.tensor_copy(out=TTt[0:64, 4:7:2, :], in_=pTTt[0:64, 0:2, :]), nowaw=True)
                    s.op('dve', [pTT], [TT], lambda e: e.tensor_copy(out=TTt[64:128, 5:8:2, :], in_=pTTt[64:128, 0:2, :]), nowaw=True)
                    fns = []
                    for h in range(4):
                        pr = h // 2
                        for bk in range(2):
                            fns.append(lambda e, h=h, pr=pr, bk=bk: e.matmul(
                                out=pSt[:, h * 256 + bk * 128:h * 256 + (bk + 1) * 128], lhsT=TTt[:, pr * 2 + bk, :],
                                rhs=TTt[:, 4 + 2 * pr + (h % 2), :], start=True, stop=True))
                    s.group('pe', [TT], [pS], fns)
                    for hh in range(2):
                        s.op('act', [pS], [Ph[2 * hh], Ph[2 * hh + 1]], lambda e, hh=hh: e.activation(
                            out=Ptt[:, 512 * hh:512 * hh + 512], in_=pSt[:, 512 * hh:512 * hh + 512], func=AF.Exp, scale=0.125))
                    mk = mask_b[:]
                    mkb = bass.AP(mk.tensor, mk.offset, [list(mk.ap[0]), [0, 4], list(mk.ap[-1])])
                    s.op('dve', Ph + ['mask_b'], Ph, lambda e: e.tensor_tensor(
                        out=Ptt[:, :].rearrange("p (h c) -> p h c", h=4), in0=Ptt[:, :].rearrange("p (h c) -> p h c", h=4),
                        in1=mkb, op=ALU.mult))
                    fns = []
                    for h in range(4):
                        for bk in range(2):
                            fns.append(lambda e, h=h, bk=bk: e.matmul(
                                out=pOt[0:65, h, :], lhsT=Vtt[:, bk, h, :], rhs=Ptt[:, h * 256 + bk * 128:h * 256 + (bk + 1) * 128],
                                start=(bk == 0), stop=(bk == 1)))
                    u['_pv'] = (fns, Vt, Ph, pO, pOt)

                def unit_stage2(idx):
                    u = units[idx]
                    fns, Vt, Ph, pO, pOt = u['_pv']
                    s.group('pe', [Vt] + Ph, [pO], fns)
                    if u['acc_res'] == 'acc':
                        s.op('dve', [pO, 'acc'], ['acc'], lambda e: e.tensor_tensor(
                            out=u['acc_view'], in0=pOt[0:65, :, :], in1=u['acc_view'], op=ALU.add))
                    else:
                        s.op('dve', [pO, 'accS'], ['accS'], lambda e: e.tensor_tensor(
                            out=u['acc_view'], in0=pOt[0:65, :, 0:1], in1=u['acc_view'], op=ALU.add))

                def normalize(acc_t, acc_res, col0, ncols, dst, dst_res):
                    if ncols <= 128:
                        pBres, pBap = 'pO0', bufs['pO0'][0:64, 0, 0:ncols]
                    else:
                        pBres, pBap = 'pS0', bufs['pS0'][0:64, 0:ncols]
                    for h in range(4):
                        s.op('dve', [acc_res], ['rr'], lambda e, h=h: e.tensor_scalar(
                            out=rr[:, 0:ncols], in0=acc_t[:, h, col0:col0 + ncols], scalar1=1e-30, scalar2=None, op0=ALU.max))
                        s.op('dve', ['rr'], ['rr'], lambda e: e.reciprocal(out=rr[:, 0:ncols], in_=rr[:, 0:ncols]))
                        s.op('pe', ['rr', 'sel_f'], [pBres], lambda e: e.matmul(
                            out=pBap, lhsT=sel_f[:, :], rhs=rr[:, 0:ncols], start=True, stop=True))
                        s.op('dve', [acc_res, pBres], ['ab'], lambda e, h=h: e.tensor_tensor(
                            out=ab[:, h, 0:ncols], in0=acc_t[0:64, h, col0:col0 + ncols], in1=pBap, op=ALU.mult))
                    s.dma('sp', 'ab', ['ab'], dst_res, lambda q: q.dma_start(out=dst, in_=ab[:, :, 0:ncols]))

                items = []
                for S0 in (range(flo, NW, 2048) if dbg is None else [flo]):
                    items.append(('f', lambda: s.op('dve', [], ['acc'], lambda e: e.memset(acc[:], 0.0))))
                    for g in range(3):
                        d = DILS[g]
                        for n in range(16 // d):
                            for r in range(d):
                                base = S0 + 128 * d * n + r
                                off = 128 * d * n + r
                                rows_c = slice(base, base + 127 * d + 1, d)
                                rows_a = slice(base - 128 * d, base + 127 * d + 1, d)
                                units.append(dict(g=g, samp=False, q=qs[rows_c, 256 * g:256 * g + 256],
                                                  k=ks[rows_a, 256 * g:256 * g + 256], v=vs[rows_a, 260 * g:260 * g + 260],
                                                  acc_res='acc', acc_view=acc[0:65, :, off:off + 127 * d + 1:d]))
                                items.append(('u', len(units) - 1))
                    for c in range(4):
                        items.append(('f', lambda c=c, S0=S0: normalize(
                            acc, 'acc', c * 512, 512,
                            ac_T[0:256, S0 + c * 512:S0 + c * 512 + 512].rearrange("(h p) n -> p h n", p=64), [])))
                zz = sb("zz", [128, 8, 128], BF16)
                s.op('dve', [], ['zz'], lambda e: e.memset(zz[:], 0.0))
                s.dma('sp', 'zz', ['zz'], ['acT_s'], lambda q: q.dma_start(
                    out=ac_T[:, NW:NW + 128].rearrange("(k p) n -> p k n", p=128), in_=zz[:]))
                for b in (range(4) if not _NOSAMP else []):
                    for g in range(3):
                        d = DILS[g]
                        row = NW + b
                        units.append(dict(g=g, samp=True, q=bcast_rows(qs[row:row + 1, 256 * g:256 * g + 256], 128),
                                          kp=ckv[g][l, b, 0, 0:127 * d + 1:d, :], k=bcast_rows(ks[row:row + 1, 256 * g:256 * g + 256], 128),
                                          vp=ckv[g][l, b, 1, 0:127 * d + 1:d, :], v=bcast_rows(vs[row:row + 1, 260 * g:260 * g + 260], 128),
                                          acc_res='accS', acc_view=accS[0:65, :, b:b + 1]))
                        items.append(('u', len(units) - 1))
                items.append(('f', lambda: normalize(accS, 'accS', 0, 4, ac_T[0:256, NW:NW + 4].rearrange("(h p) n -> p h n", p=64), ['acT_s'])))
                nload = [0]
                pend = [None]
                for it in items:
                    if it[0] == 'u':
                        while nload[0] <= min(it[1] + _PF, len(units) - 1):
                            unit_load(nload[0])
                            nload[0] += 1
                        unit_compute(it[1])
                        if pend[0] is not None:
                            unit_stage2(pend[0])
                        pend[0] = it[1]
                    else:
                        if pend[0] is not None:
                            unit_stage2(pend[0])
                            pend[0] = None
                        it[1]()
                if pend[0] is not None:
                    unit_stage2(pend[0])
                s.barrier()
            if _STOP_AFTER == (l, 'B1'):
                s.barrier(final=True)
                return nc

            with ExitStack() as st:
                sb = lambda n, sh, dt: st.enter_context(nc.sbuf_tensor(un(n), sh, dt))
                ps = lambda n, sh, dt: st.enter_context(nc.psum_tensor(un(n), sh, dt))
                dg = sb("dg", [128, 6, 31, 128], BF16)
                cw = sb("cw", [128, 6, 31], F32)
                cp = sb("cp", [128, 3, 6], F32)
                s.dma('sp', 'cw', [], ['cw'], lambda q: q.dma_start(out=cw[:], in_=convwT[l]))
                s.dma('sp', 'cp', [], ['cp'], lambda q: q.dma_start(out=cp[:], in_=convp[l]))
                for cc in range(6):
                    for j in range(31):
                        if (cc * 31 + j) % 2 == 0:
                            s.op('dve', ['cw', 'ident_b'], ['dg'], lambda e, cc=cc, j=j: e.tensor_scalar(
                                out=dg[:, cc, j, :], in0=ident_b[:], scalar1=cw[:, cc, j:j + 1], scalar2=None, op0=ALU.mult), nowaw=True)
                        else:
                            s.op('act', ['cw', 'ident_b'], ['dg'], lambda e, cc=cc, j=j: e.activation(
                                out=dg[:, cc, j, :], in_=ident_b[:], func=AF.Copy, scale=cw[:, cc, j:j + 1]), nowaw=True)
                for i in range(2):
                    bufs['uc%d' % i] = sb("uc%d" % i, [128, 6, 544], BF16)
                    bufs['cvb%d' % i] = sb("cvb%d" % i, [128, 6, 512], BF16)
                    bufs['pC%d' % i] = ps("pC%d" % i, [128, 512], F32)
                cfs = [sb("cf%d" % i, [128, 6, 512], F32) for i in range(2)]
                cfi = [0]
                cfb = sb("cfb", [128, 6, 512], BF16)
                sq = sb("sq", [128, 6, 512], BF16)
                pM = ps("pM", [128, 512], F32)
                pQ = ps("pQ", [128, 512], F32)
                pTr = ps("pTr", [128, 128], F32)
                mean = sb("mean", [128, 512], F32)
                msq = sb("msq", [128, 512], F32)
                var = sb("var", [128, 512], F32)
                rs = sb("rs", [128, 512], F32)
                mb = sb("mb", [128, 512], F32)
                tt = sb("tt", [128, 512], F32)
                stt = sb("stt", [128, DQ], F32)
                s.op('dve', [], ['stt'], lambda e: e.memset(stt[:], 0.0))
                usn = sb("usn", [128, 6, 4], BF16)

                def conv_tile(uc, N, cvb):
                    uct, cvbt = bufs[uc], bufs[cvb]
                    cf = cfs[cfi[0] % 2]
                    cfr = 'cf%d' % (cfi[0] % 2)
                    cfi[0] += 1
                    for cc in range(6):
                        pC = 'pC%d' % (cc % 2)
                        s.group('pe', [uc, 'dg'], [pC], [
                            (lambda e, j=j, cc=cc, pC=pC: e.matmul(out=bufs[pC][:, 0:N], lhsT=dg[:, cc, j, :], rhs=uct[:, cc, j:j + N],
                                                                  start=(j == 0), stop=(j == 30))) for j in range(31)])
                        s.op('act', [pC, 'cp'], [cfr], lambda e, cc=cc, pC=pC: e.activation(
                            out=cf[:, cc, 0:N], in_=bufs[pC][:, 0:N], func=AF.Identity, bias=cp[:, 0, cc:cc + 1]))
                        s.op('pool', [cfr], ['cfb'], lambda e, cc=cc: e.tensor_copy(out=cfb[:, cc, 0:N], in_=cf[:, cc, 0:N]))
                        s.op('dve', [cfr], ['sq'], lambda e, cc=cc: e.tensor_tensor(
                            out=sq[:, cc, 0:N], in0=cf[:, cc, 0:N], in1=cf[:, cc, 0:N], op=ALU.mult))
                    s.group('pe', ['cfb', 'ones_b'], ['pM'], [
                        (lambda e, cc=cc: e.matmul(out=pM[:, 0:N], lhsT=ones_b[:], rhs=cfb[:, cc, 0:N], start=(cc == 0), stop=(cc == 5)))
                        for cc in range(6)])
                    s.group('pe', ['sq', 'ones_b'], ['pQ'], [
                        (lambda e, cc=cc: e.matmul(out=pQ[:, 0:N], lhsT=ones_b[:], rhs=sq[:, cc, 0:N], start=(cc == 0), stop=(cc == 5)))
                        for cc in range(6)])
                    s.op('dve', ['pM'], ['mean'], lambda e: e.tensor_scalar(out=mean[:, 0:N], in0=pM[:, 0:N], scalar1=1.0 / DQ, scalar2=None, op0=ALU.mult))
                    s.op('dve', ['mean'], ['msq'], lambda e: e.tensor_tensor(out=msq[:, 0:N], in0=mean[:, 0:N], in1=mean[:, 0:N], op=ALU.mult))
                    s.op('dve', ['pQ', 'msq'], ['var'], lambda e: e.scalar_tensor_tensor(
                        out=var[:, 0:N], in0=pQ[:, 0:N], scalar=1.0 / DQ, in1=msq[:, 0:N], op0=ALU.mult, op1=ALU.subtract))
                    s.op('dve', ['var'], ['var'], lambda e: e.tensor_scalar(out=var[:, 0:N], in0=var[:, 0:N], scalar1=LN_EPS, scalar2=None, op0=ALU.add))
                    s.op('act', ['var'], ['rs'], lambda e: e.activation(out=rs[:, 0:N], in_=var[:, 0:N], func=AF.Sqrt))
                    s.op('dve', ['rs'], ['rs'], lambda e: e.reciprocal(out=rs[:, 0:N], in_=rs[:, 0:N]))
                    s.op('dve', ['rs', 'mean'], ['mb'], lambda e: e.tensor_tensor(out=mb[:, 0:N], in0=mean[:, 0:N], in1=rs[:, 0:N], op=ALU.mult))
                    for cc in range(6):
                        s.op('dve', [cfr, 'rs'], ['tt'], lambda e, cc=cc: e.tensor_tensor(out=tt[:, 0:N], in0=cf[:, cc, 0:N], in1=rs[:, 0:N], op=ALU.mult))
                        s.op('dve', ['tt', 'mb'], ['tt'], lambda e: e.tensor_tensor(out=tt[:, 0:N], in0=tt[:, 0:N], in1=mb[:, 0:N], op=ALU.subtract))
                        s.op('act', ['tt', 'cp'], [cvb], lambda e, cc=cc: e.activation(
                            out=cvbt[:, cc, 0:N], in_=tt[:, 0:N], func=AF.Silu, scale=cp[:, 1, cc:cc + 1], bias=cp[:, 2, cc:cc + 1]))

                tlB = list(range(flo, NW, 512))

                def loadB(ti):
                    c0 = tlB[ti]
                    uc = 'uc%d' % (ti % 2)
                    s.dma('sp', uc, [], [uc], lambda q: q.dma_start(
                        out=bufs[uc][:, :, 0:542], in_=us_T[:, c0 - 30:c0 + 512].rearrange("(cc p) n -> p cc n", p=128)))
                loadB(0)
                for ti, c0 in enumerate(tlB):
                    if ti + 1 < len(tlB):
                        loadB(ti + 1)
                    uc, cvb = 'uc%d' % (ti % 2), 'cvb%d' % (ti % 2)
                    conv_tile(uc, 512, cvb)
                    s.dma('sp', cvb, [cvb], [], lambda q, cvb=cvb, c0=c0: q.dma_start(
                        out=ac_T[256:1024, c0:c0 + 512].rearrange("(cc p) n -> p cc n", p=128), in_=bufs[cvb][:, :, :]))
                uc, cvb = 'uc0', 'cvb0'
                s.op('dve', [], [uc], lambda e: e.memset(bufs[uc][:], 0.0))
                s.dma('sp', 'usn', [], ['usn'], lambda q: q.dma_start(
                    out=usn[:], in_=us_T[:, NW:NW + 4].rearrange("(cc p) n -> p cc n", p=128)))
                for b in range(4):
                    s.dma('sp', 'stt', [], ['stt'], lambda q, b=b: q.dma_start(out=stt[0:30, :], in_=sconv[l, b]))
                    for cc in range(6):
                        s.op('pe', ['stt', 'ident_f'], ['pTr'], lambda e, cc=cc: e.transpose(
                            out=pTr[:, :], in_=stt[:, cc * 128:(cc + 1) * 128], identity=ident_f[:]))
                        s.op('act', ['pTr'], [uc], lambda e, cc=cc, b=b: e.activation(
                            out=bufs[uc][:, cc, 32 * b:32 * b + 30], in_=pTr[:, 0:30], func=AF.Copy))
                s.op('dve', ['usn'], [uc], lambda e: e.tensor_copy(out=bufs[uc][:, :, 30:127:32], in_=usn[:, :, :]))
                conv_tile(uc, 128, cvb)
                cvs = sb("cvs", [128, 6, 4], BF16)
                s.op('dve', [cvb], ['cvs'], lambda e: e.tensor_copy(out=cvs[:], in_=bufs[cvb][:, :, 0:97:32]))
                s.dma('sp', 'cvs', ['cvs'], [], lambda q: q.dma_start(
                    out=ac_T[256:1024, NW:NW + 4].rearrange("(cc p) n -> p cc n", p=128), in_=cvs[:]))
                s.barrier()
            if _STOP_AFTER == (l, 'B2'):
                s.barrier(final=True)
                return nc

            with ExitStack() as st:
                sb = lambda n, sh, dt: st.enter_context(nc.sbuf_tensor(un(n), sh, dt))
                ps = lambda n, sh, dt: st.enter_context(nc.psum_tensor(un(n), sh, dt))
                wo = load_weight_cols(st, "wo", w_o[l], 8, D, 512)
                stn = {'ss': sb("ss", [128, 4], F32), 'rstd': sb("rstd", [128, 4], F32),
                       'junk': sb("junk", [128, D], F32), 'xn': sb("xn", [128, 4, D], BF16)}
                for i in range(2):
                    bufs['xtC%d' % i] = sb("xtC%d" % i, [128, 4, D], F32)
                    bufs['acC%d' % i] = sb("acC%d" % i, [128, 8, 512], BF16)
                    bufs['htC%d' % i] = sb("htC%d" % i, [128, 4, D], F32)
                    bufs['n2C%d' % i] = sb("n2C%d" % i, [128, 8, 512], BF16)
                    bufs['pTC%d' % i] = ps("pTC%d" % i, [128, 4, 128], BF16)
                for i in range(4):
                    bufs['pH%d' % i] = ps("pH%d" % i, [128, 512], F32)
                tlC = tiles_for(flo)

                def loadC1(ti):
                    c0, nb = tlC[ti]
                    xt, acn = 'xtC%d' % (ti % 2), 'acC%d' % (ti % 2)
                    s.dma('sp', acn, [], [acn], lambda q: q.dma_start(
                        out=bufs[acn][:, :, 0:nb * 128], in_=ac_T[:, c0:c0 + nb * 128].rearrange("(k p) n -> p k n", p=128)))
                    s.dma('sp', xt, [], [xt], lambda q: q.dma_start(
                        out=bufs[xt][:, 0:nb, :], in_=xin[c0:c0 + nb * 128, :].rearrange("(b p) d -> p b d", p=128)))
                def mmC1(ti):
                    c0, nb = tlC[ti]
                    xt, acn, ht = 'xtC%d' % (ti % 2), 'acC%d' % (ti % 2), 'htC%d' % (ti % 2)
                    xtt, act_, htt = bufs[xt], bufs[acn], bufs[ht]
                    for b in range(nb):
                        for hf in range(2):
                            pH = 'pH%d' % ((b * 2 + hf) % 4)
                            s.group('pe', [acn, 'wo%d' % hf], [pH], [
                                (lambda e, k=k, b=b, hf=hf, pH=pH: e.matmul(
                                    out=bufs[pH][:], lhsT=act_[:, k, b * 128:(b + 1) * 128], rhs=wo[:, k, hf * 512:(hf + 1) * 512],
                                    start=(k == 0), stop=(k == 7))) for k in range(8)])
                            s.op('dve', [pH, xt], [ht], lambda e, b=b, hf=hf, pH=pH: e.tensor_tensor(
                                out=htt[:, b, hf * 512:(hf + 1) * 512], in0=bufs[pH][:], in1=xtt[:, b, hf * 512:(hf + 1) * 512], op=ALU.add))

                def tailC1(ti):
                    c0, nb = tlC[ti]
                    N = nb * 128
                    ht, n2 = 'htC%d' % (ti % 2), 'n2C%d' % (ti % 2)
                    htt, n2t = bufs[ht], bufs[n2]
                    s.dma('sp', ht, [ht], [], lambda q: q.dma_start(
                        out=hs[c0:c0 + nb * 128, :].rearrange("(b p) d -> p b d", p=128), in_=htt[:, 0:nb, :]))
                    norm_transpose(stn, None, ht, nb, ('gffn', gffn_sb[:, l, :]), n2, 'C', ['pTC0', 'pTC1'])
                    s.dma('sp', n2, [n2], [], lambda q: q.dma_start(
                        out=n2_T[:, c0:c0 + N].rearrange("(k p) n -> p k n", p=128), in_=n2t[:, :, 0:N]))

                loadC1(0)
                if len(tlC) > 1:
                    loadC1(1)
                mmC1(0)
                for ti in range(len(tlC)):
                    if ti + 2 < len(tlC):
                        loadC1(ti + 2)
                    if ti + 1 < len(tlC):
                        mmC1(ti + 1)
                    tailC1(ti)
                s.barrier()
            if _STOP_AFTER == (l, 'C1'):
                s.barrier(final=True)
                return nc

            with ExitStack() as st:
                sb = lambda n, sh, dt: st.enter_context(nc.sbuf_tensor(un(n), sh, dt))
                ps = lambda n, sh, dt: st.enter_context(nc.psum_tensor(un(n), sh, dt))
                wu = load_weight_cols(st, "wu", w_up[l], 8, DFF, 1024)
                for i in range(2):
                    bufs['n2U%d' % i] = sb("n2U%d" % i, [128, 8, 512], BF16)
                    bufs['fU%d' % i] = sb("fU%d" % i, [128, 32, 512], BF16)
                    bufs['rl%d' % i] = sb("rl%d" % i, [128, 512], F32)
                for i in range(4):
                    bufs['pF%d' % i] = ps("pF%d" % i, [128, 512], F32)
                tlU = tiles_for(flo)

                def loadU(ti):
                    c0, nb = tlU[ti]
                    n2 = 'n2U%d' % (ti % 2)
                    s.dma('sp', n2, [], [n2], lambda q: q.dma_start(
                        out=bufs[n2][:, :, 0:nb * 128], in_=n2_T[:, c0:c0 + nb * 128].rearrange("(k p) n -> p k n", p=128)))
                loadU(0)
                for ti, (c0, nb) in enumerate(tlU):
                    if ti + 1 < len(tlU):
                        loadU(ti + 1)
                    N = nb * 128
                    n2, fU = 'n2U%d' % (ti % 2), 'fU%d' % (ti % 2)
                    n2t, fUt = bufs[n2], bufs[fU]
                    for m in range(32):
                        pF, rl = 'pF%d' % (m % 4), 'rl%d' % (m % 2)
                        s.group('pe', [n2, 'wu%d' % (m // 8)], [pF], [
                            (lambda e, k=k, m=m, pF=pF: e.matmul(out=bufs[pF][:, 0:N], lhsT=wu[:, k, m * 128:(m + 1) * 128],
                                                                rhs=n2t[:, k, 0:N], start=(k == 0), stop=(k == 7))) for k in range(8)])
                        s.op('act', [pF], [rl], lambda e, pF=pF, rl=rl: e.activation(out=bufs[rl][:, 0:N], in_=bufs[pF][:, 0:N], func=AF.Relu))
                        s.op('dve', [rl], [fU], lambda e, m=m, rl=rl: e.tensor_tensor(
                            out=fUt[:, m, 0:N], in0=bufs[rl][:, 0:N], in1=bufs[rl][:, 0:N], op=ALU.mult), nowaw=(m > 0))
                    s.dma('sp', fU, [fU], [], lambda q, fUt=fUt, c0=c0, N=N: q.dma_start(
                        out=f_T[:, c0:c0 + N].rearrange("(m p) n -> p m n", p=128), in_=fUt[:, :, 0:N]))
                    if l == 0:
                        emit_shift_some(9 if ti + 1 < len(tlU) else 1000)
                s.barrier()
            if _STOP_AFTER == (l, 'C2a'):
                s.barrier(final=True)
                return nc

            with ExitStack() as st:
                sb = lambda n, sh, dt: st.enter_context(nc.sbuf_tensor(un(n), sh, dt))
                ps = lambda n, sh, dt: st.enter_context(nc.psum_tensor(un(n), sh, dt))
                wd = load_weight_cols(st, "wd", w_down[l], 32, D, 512)
                ss2 = sb("ss2", [128, 4], F32)
                rstd2 = sb("rstd2", [128, 4], F32)
                junk2 = sb("junk2", [128, D], F32)
                for i in range(2):
                    bufs['fD%d' % i] = sb("fD%d" % i, [128, 32, 512], BF16)
                    bufs['hD%d' % i] = sb("hD%d" % i, [128, 4, D], F32)
                for i in range(4):
                    bufs['pY%d' % i] = ps("pY%d" % i, [128, 512], F32)
                tlD = tiles_for(flo)

                def loadD(ti):
                    c0, nb = tlD[ti]
                    fD, hD = 'fD%d' % (ti % 2), 'hD%d' % (ti % 2)
                    s.dma('sp', fD, [], [fD], lambda q: q.dma_start(
                        out=bufs[fD][:, :, 0:nb * 128], in_=f_T[:, c0:c0 + nb * 128].rearrange("(m p) n -> p m n", p=128)))
                    s.dma('sp', hD, [], [hD], lambda q: q.dma_start(
                        out=bufs[hD][:, 0:nb, :], in_=hs[c0:c0 + nb * 128, :].rearrange("(b p) d -> p b d", p=128)))
                loadD(0)
                for ti, (c0, nb) in enumerate(tlD):
                    if ti + 1 < len(tlD):
                        loadD(ti + 1)
                    N = nb * 128
                    fD, hD = 'fD%d' % (ti % 2), 'hD%d' % (ti % 2)
                    yD = hD
                    fDt, hDt = bufs[fD], bufs[hD]
                    yDt = hDt
                    for b in range(nb):
                        for hf in range(2):
                            pY = 'pY%d' % ((b * 2 + hf) % 4)
                            s.group('pe', [fD, 'wd%d' % hf], [pY], [
                                (lambda e, m=m, b=b, hf=hf, pY=pY: e.matmul(
                                    out=bufs[pY][:], lhsT=fDt[:, m, b * 128:(b + 1) * 128], rhs=wd[:, m, hf * 512:(hf + 1) * 512],
                                    start=(m == 0), stop=(m == 31))) for m in range(32)])
                            s.op('dve', [pY, hD], [yD], lambda e, b=b, hf=hf, pY=pY: e.tensor_tensor(
                                out=yDt[:, b, hf * 512:(hf + 1) * 512], in0=bufs[pY][:], in1=hDt[:, b, hf * 512:(hf + 1) * 512], op=ALU.add))
                    if l == 0:
                        for b in range(nb):
                            blk = c0 // 128 + b
                            s.op('dve', [yD, 'valid'], [yD], lambda e, b=b, blk=blk: e.tensor_scalar(
                                out=yDt[:, b, :], in0=yDt[:, b, :], scalar1=valid_sb[:, blk:blk + 1], scalar2=None, op0=ALU.mult))
                        s.dma('sp', yD, [yD], [], lambda q, yDt=yDt, c0=c0, nb=nb: q.dma_start(
                            out=x1[c0:c0 + nb * 128, :].rearrange("(b p) d -> p b d", p=128), in_=yDt[:, 0:nb, :]))
                    else:
                        for b in range(nb):
                            s.op('act', [yD], ['junk2', 'ss2'], lambda e, b=b: e.activation(
                                out=junk2[:], in_=yDt[:, b, :], func=AF.Square, accum_out=ss2[:, b:b + 1]))
                        s.op('dve', ['ss2'], ['rstd2'], lambda e: e.tensor_scalar(
                            out=rstd2[:, 0:nb], in0=ss2[:, 0:nb], scalar1=1.0 / D, scalar2=RMS_EPS, op0=ALU.mult, op1=ALU.add))
                        s.op('act', ['rstd2'], ['rstd2'], lambda e: e.activation(out=rstd2[:, 0:nb], in_=rstd2[:, 0:nb], func=AF.Sqrt))
                        s.op('dve', ['rstd2'], ['rstd2'], lambda e: e.reciprocal(out=rstd2[:, 0:nb], in_=rstd2[:, 0:nb]))
                        for b in range(nb):
                            s.op('dve', [yD, 'rstd2', 'gfin'], [yD], lambda e, b=b: e.scalar_tensor_tensor(
                                out=yDt[:, b, :], in0=yDt[:, b, :], scalar=rstd2[:, b:b + 1], in1=gfin_sb[:], op0=ALU.mult, op1=ALU.mult))
                        o0 = c0 - 4096
                        s.dma('sp', yD, [yD], [], lambda q, yDt=yDt, o0=o0, nb=nb: q.dma_start(
                            out=y_o[o0:o0 + nb * 128, :].rearrange("(b p) d -> p b d", p=128), in_=yDt[:, 0:nb, :]))
                s.barrier()
            if _STOP_AFTER == (l, 'C2b'):
                s.barrier(final=True)
                return nc
        s.barrier(final=True)
    return nc


def _rope_tables(pos):
    inv = (1.0 / (np.float32(500000.0) ** (np.arange(0, 16, 2, dtype=np.float32) / np.float32(16)))).astype(np.float32)
    ang = (pos.astype(np.float32)[:, None] * inv[None, :]).astype(np.float32)
    return np.cos(ang).astype(np.float32), np.sin(ang).astype(np.float32)


def kernel(x_prompt, x_sample, cache_kv_w128, cache_kv_w512, cache_kv_w2048, state_conv,
           w_in, w_o, conv_w, conv_b, conv_ln_g, conv_ln_b, norm_mix, norm_ffn, w_up, w_down, norm_final):
    f32 = np.float32
    x_prompt = np.asarray(x_prompt, f32)
    x_sample = np.asarray(x_sample, f32)
    caches = [np.asarray(c, f32) for c in (cache_kv_w128, cache_kv_w512, cache_kv_w2048)]
    state_conv = np.asarray(state_conv, f32)
    w_in = np.ascontiguousarray(np.asarray(w_in, f32))
    w_o = np.ascontiguousarray(np.asarray(w_o, f32))
    w_up = np.ascontiguousarray(np.asarray(w_up, f32))
    w_down = np.ascontiguousarray(np.asarray(w_down, f32))
    conv_w = np.asarray(conv_w, f32)
    nblk = NT // 128

    def pcc(v):
        return np.ascontiguousarray(np.asarray(v, f32).reshape(6, 128).T)

    convwT = np.stack([np.ascontiguousarray(conv_w[l].T.reshape(6, 128, 31).transpose(1, 0, 2)) for l in range(2)])
    convp = np.stack([np.stack([pcc(conv_b[l]), pcc(conv_ln_g[l]), pcc(conv_ln_b[l])], axis=1) for l in range(2)])
    gmix = np.stack([np.ascontiguousarray(np.asarray(norm_mix, f32)[l].reshape(8, 128).T) for l in range(2)])
    gffn = np.stack([np.ascontiguousarray(np.asarray(norm_ffn, f32)[l].reshape(8, 128).T) for l in range(2)])
    gfin = np.asarray(norm_final, f32).reshape(1, D)
    ident = np.eye(128, dtype=f32)
    kk = np.arange(128)[:, None]
    qq = np.arange(128)[None, :]
    mask = np.concatenate([(qq <= kk), (kk <= qq)], axis=1).astype(f32)

    in_maps = []
    for c in range(8):
        sq_, half = c // 2, c % 2
        w0 = -4096 if half == 0 else 0
        xwin = np.zeros((NT, D), f32)
        pos = np.zeros((NT,), np.int64)
        valid = np.zeros((NT,), f32)
        p = w0 + np.arange(NW)
        real = p >= 0
        xwin[:NW][real] = x_prompt[sq_, p[real]]
        pos[:NW][real] = p[real]
        valid[:NW][real] = 1.0
        xwin[NW:NW + 4] = x_sample[4 * c:4 * c + 4, 0]
        pos[NW:] = 16384
        valid[NW:NW + 4] = 1.0
        cos, sin = _rope_tables(pos)
        to_pb = lambda a: np.ascontiguousarray(a.reshape(nblk, 128, *a.shape[1:]).swapaxes(0, 1))
        m = {
            "xw": xwin, "validT": to_pb(valid), "cosT": to_pb(cos), "sinT": to_pb(sin),
            "ident": ident, "mask": mask, "w_in": w_in, "w_o": w_o, "w_up": w_up, "w_down": w_down,
            "convwT": convwT, "convp": convp, "gmix": gmix, "gffn": gffn, "gfin": gfin,
            "sconv": np.ascontiguousarray(state_conv[:, 4 * c:4 * c + 4]),
        }
        for g in range(3):
            m["ckv%d" % g] = np.ascontiguousarray(caches[g][:, 4 * c:4 * c + 4].reshape(2, 4, 2, NBUF[g], 256))
        in_maps.append(m)

    nc = build_program()
    res = run_bass_kernel_spmd(nc, in_maps, core_ids=list(range(8)))
    R = res.results

    y_prompt = np.zeros((4, 8192, D), f32)
    y_sample = np.zeros((32, 1, D), f32)
    kvp = [np.zeros((2, 4, 2, NBUF[g], 4, 64), f32) for g in range(3)]
    convp_out = np.zeros((2, 4, 30, DQ), f32)
    kvs = [np.zeros((2, 32, 2, NBUF[g], 4, 64), f32) for g in range(3)]
    convs_out = np.zeros((2, 32, 30, DQ), f32)
    for c in range(8):
        sq_, half = c // 2, c % 2
        r = R[c]
        y_prompt[sq_, half * 4096:(half + 1) * 4096] = r["y"][0:4096]
        y_sample[4 * c:4 * c + 4, 0] = r["y"][4096:4100]
        if half == 1:
            for g in range(3):
                n = NBUF[g]
                kvp[g][:, sq_, 0] = r["kout"][:, 2048 - n:, 256 * g:256 * g + 256].reshape(2, n, 4, 64)
                kvp[g][:, sq_, 1] = r["vout"][:, 2048 - n:, 256 * g:256 * g + 256].reshape(2, n, 4, 64)
            convp_out[:, sq_] = r["convp_o"].transpose(0, 2, 1)
        for g in range(3):
            kvs[g][:, 4 * c:4 * c + 4] = r["okv%d" % g].reshape(2, 4, 2, NBUF[g], 4, 64)
        convs_out[:, 4 * c:4 * c + 4, 0:29] = r["convs_o"]
        convs_out[:, 4 * c:4 * c + 4, 29] = r["convs_new"].transpose(0, 2, 1)
    return (y_prompt, y_sample, kvp[0], kvp[1], kvp[2], convp_out, kvs[0], kvs[1], kvs[2], convs_out)
```

```python
from contextlib import ExitStack
import numpy as np
import concourse.bass as bass
import concourse.mybir as mybir
from concourse.bass_utils import run_bass_kernel_spmd

F32 = mybir.dt.float32
BF16 = mybir.dt.bfloat16
AF = mybir.ActivationFunctionType
ALU = mybir.AluOpType

D = 1024
NW = 8192
NT = NW + 128
DQ = 768
DIN = 3840
DFF = 4096
DILS = (1, 4, 16)
NBUF = (128, 512, 2048)
RMS_EPS = 1e-6
_SKIP_SHIFT = False
_CUT = 99
_PF = 2
_HSEL = (0, 1, 2, 3)
_BSEL = (0, 1)
_NOSAMP = False
_STOP_AFTER = None
LN_EPS = 1e-5


class Sch:
    def __init__(self, nc):
        self.nc = nc
        self.eng = {'pe': nc.tensor, 'act': nc.scalar, 'dve': nc.vector, 'pool': nc.gpsimd, 'sp': nc.sync}
        self.sem = {e: nc.alloc_semaphore(name="s_" + e) for e in ['pe', 'act', 'dve', 'pool']}
        self.cnt = {e: 0 for e in self.sem}
        self.known = {e: {} for e in self.eng}
        self.res = {}
        self.dsem = {}
        self.log = []

    def _wait(self, e, toks):
        best = {}
        for (k, h, v) in toks:
            if k == e and e in self.cnt:
                if e == 'pe':
                    continue
                if v <= self.cnt[e] - 3:
                    continue
            if self.known[e].get(k, 0) >= v:
                continue
            if k not in best or best[k][1] < v:
                best[k] = (h, v)
        for k, (h, v) in best.items():
            self.eng[e].wait_ge(h, v)
            self.known[e][k] = v
            self.log.append(('wait', e, k, v))

    def _deps(self, reads, writes, nowaw=False):
        toks = []
        for r in reads:
            st = self.res.get(r)
            if st:
                toks.extend(st['w'])
        for w in writes:
            st = self.res.get(w)
            if st:
                if not nowaw:
                    toks.extend(st['w'])
                else:
                    toks.extend(st.get('pr', []))
                toks.extend(st['r'])
        return toks

    def _commit(self, tok, reads, writes, nowaw=False):
        for r in reads:
            st = self.res.setdefault(r, {'w': [], 'r': []})
            st['r'] = [t for t in st['r'] if t[0] != tok[0]] + [tok]
        for w in writes:
            if nowaw and w in self.res:
                st = self.res[w]
                st['w'] = [t for t in st['w'] if t[0] != tok[0]] + [tok]
            else:
                old = self.res.get(w)
                self.res[w] = {'w': [tok], 'r': [], 'pr': (old['r'] if old else [])}

    def op(self, e, reads, writes, fn, nowaw=False):
        self.group(e, reads, writes, [fn], nowaw)

    def group(self, e, reads, writes, fns, nowaw=False):
        self._wait(e, self._deps(reads, writes, nowaw))
        ins = None
        for fn in fns:
            ins = fn(self.eng[e])
        self.cnt[e] += 1
        ins.then_inc(self.sem[e], 1)
        self._commit((e, self.sem[e], self.cnt[e]), reads, writes, nowaw)
        self.log.append(('op', e, self.cnt[e], tuple(reads), tuple(writes)))

    def dma(self, q, semkey, reads, writes, fn, nowaw=False):
        self._wait(q, self._deps(reads, writes, nowaw))
        if semkey not in self.dsem:
            self.dsem[semkey] = [self.nc.alloc_semaphore(name="d_" + str(semkey)), 0]
        ds = self.dsem[semkey]
        ins = fn(self.eng[q])
        ds[1] += 16
        ins.then_inc(ds[0], 16)
        self._commit((('d', semkey), ds[0], ds[1]), reads, writes, nowaw)
        self.log.append(('dma', q, semkey, ds[1], tuple(reads), tuple(writes)))

    def barrier(self, final=False):
        toks = [(e, self.sem[e], self.cnt[e]) for e in self.sem if self.cnt[e] > 0]
        toks += [(('d', k), v[0], v[1]) for k, v in self.dsem.items() if v[1] > 0 and (final or k != 'shift')]
        for e in self.eng:
            for (k, h, v) in toks:
                if k == e:
                    continue
                if self.known[e].get(k, 0) >= v:
                    continue
                self.eng[e].wait_ge(h, v)
                self.known[e][k] = v
        self.res = {}


def bcast_rows(ap2d, n):
    a = ap2d.ap
    return bass.AP(ap2d.tensor, ap2d.offset, [[0, n], [a[-1][0], a[-1][1]]])


def build_program(dbg=None):
    nc = bass.Bass("TRN2", target_bir_lowering=False)
    din = lambda n, sh: nc.dram_tensor(n, sh, F32, kind="ExternalInput").ap()
    dout = lambda n, sh: nc.dram_tensor(n, sh, F32, kind="ExternalOutput").ap()
    dscr = lambda n, sh, dt: nc.dram_tensor(n, sh, dt).ap()

    xw = din("xw", [NT, D] if dbg is None else [128, D])
    validT = din("validT", [128, NT // 128])
    cosT = din("cosT", [128, NT // 128, 8])
    sinT = din("sinT", [128, NT // 128, 8])
    ident_d = din("ident", [128, 128])
    mask_d = din("mask", [128, 256])
    w_in = din("w_in", [2, D, DIN] if dbg is None else [2, 128, 128])
    w_o = din("w_o", [2, D, D] if dbg is None else [2, 128, 128])
    w_up = din("w_up", [2, D, DFF] if dbg is None else [2, 128, 128])
    w_down = din("w_down", [2, DFF, D] if dbg is None else [2, 128, 128])
    convwT = din("convwT", [2, 128, 6, 31])
    convp = din("convp", [2, 128, 3, 6])
    gmix = din("gmix", [2, 128, 8])
    gffn = din("gffn", [2, 128, 8])
    gfin = din("gfin", [1, D])
    ckv = [din("ckv%d" % g, [2, 4, 2, NBUF[g], 256]) for g in range(3)]
    sconv = din("sconv", [2, 4, 30, DQ])

    y_o = dout("y", [4096 + 128, D])
    kout = dout("kout", [2, 2048, DQ])
    vout = dout("vout", [2, 2048, DQ])
    convp_o = dout("convp_o", [2, DQ, 30])
    okv = [dout("okv%d" % g, [2, 4, 2, NBUF[g], 256]) for g in range(3)]
    convs_o = dout("convs_o", [2, 4, 29, DQ])
    convs_new = dout("convs_new", [2, DQ, 4])

    if dbg == 'B1':
        qs = nc.dram_tensor("qs", [NT, DQ], BF16, kind="ExternalInput").ap()
        ks = nc.dram_tensor("ks", [NT, DQ], BF16, kind="ExternalInput").ap()
        vs = nc.dram_tensor("vs", [NT, 780], BF16, kind="ExternalInput").ap()
    else:
        qs = dscr("qs", [NT, DQ], BF16)
        ks = dscr("ks", [NT, DQ], BF16)
        vs = dscr("vs", [NT, 780], BF16)
    us_T = dscr("us_T", [DQ, NT] if dbg is None else [128, 128], BF16)
    if dbg == 'B1':
        ac_T = nc.dram_tensor("ac_T", [D, NT], BF16, kind="ExternalOutput").ap()
    else:
        ac_T = dscr("ac_T", [D, NT], BF16)
    hs = dscr("hs", [NT, D] if dbg is None else [128, 128], F32)
    n2_T = dscr("n2_T", [D, NT] if dbg is None else [128, 128], BF16)
    f_T = dscr("f_T", [DFF, NT] if dbg is None else [128, 128], BF16)
    x1 = dscr("x1", [NT, D] if dbg is None else [128, 128], F32)

    s = Sch(nc)
    nc._sch = s
    uid = [0]

    def un(n):
        uid[0] += 1
        return "%s_%d" % (n, uid[0])

    def tiles_for(lo):
        t = [(c0, 4) for c0 in range(lo, NW, 512)]
        t.append((NW, 1))
        return t

    with ExitStack() as top:
        sbt = lambda n, sh, dt: top.enter_context(nc.sbuf_tensor(n, sh, dt))
        ident_b = sbt("ident_b", [128, 128], BF16)
        ident_f = sbt("ident_f", [128, 128], F32)
        mask_b = sbt("mask_b", [128, 256], BF16)
        ones_b = sbt("ones_b", [128, 128], BF16)
        ones_f = sbt("ones_f", [128, 64], F32)
        valid_sb = sbt("valid_sb", [128, NT // 128], F32)
        cos_sb = sbt("cos_sb", [128, NT // 128, 8], F32)
        sin_sb = sbt("sin_sb", [128, NT // 128, 8], F32)
        gmix_sb = sbt("gmix_sb", [128, 2, 8], F32)
        gffn_sb = sbt("gffn_sb", [128, 2, 8], F32)
        gfin_sb = sbt("gfin_sb", [128, D], F32)

        s.dma('pool', 'c0', [], ['ident_b'], lambda q: q.dma_start(out=ident_b[:], in_=ident_d))
        s.dma('sp', 'c1', [], ['ident_f'], lambda q: q.dma_start(out=ident_f[:], in_=ident_d))
        s.dma('pool', 'c0', [], ['mask_b'], lambda q: q.dma_start(out=mask_b[:], in_=mask_d))
        s.dma('sp', 'c1', [], ['valid'], lambda q: q.dma_start(out=valid_sb[:], in_=validT))
        s.dma('sp', 'c1', [], ['cos'], lambda q: q.dma_start(out=cos_sb[:], in_=cosT))
        s.dma('sp', 'c1', [], ['sin'], lambda q: q.dma_start(out=sin_sb[:], in_=sinT))
        for l in range(2):
            s.dma('sp', 'c1', [], ['gmix%d' % l], lambda q, l=l: q.dma_start(out=gmix_sb[:, l, :], in_=gmix[l]))
            s.dma('sp', 'c1', [], ['gffn%d' % l], lambda q, l=l: q.dma_start(out=gffn_sb[:, l, :], in_=gffn[l]))
        s.dma('sp', 'c1', [], ['gfin'], lambda q: q.dma_start(out=gfin_sb[:], in_=bcast_rows(gfin, 128)))
        s.op('dve', [], ['ones_b'], lambda e: e.memset(ones_b[:], 1.0))
        s.op('dve', [], ['ones_f'], lambda e: e.memset(ones_f[:], 1.0))
        shift_q = []

        def emit_shift_some(nmax):
            for _ in range(min(nmax, len(shift_q))):
                shift_q.pop(0)()

        def emit_shift_copies():
            for g in range(3):
                n = NBUF[g]
                for l in range(2):
                    for b4 in range(4):
                        for kv in range(2):
                            if _SKIP_SHIFT:
                                continue
                            m = n - 16
                            shift_q.append(lambda g=g, l=l, m=m, b4=b4, kv=kv: s.dma('sp', 'shift', [], [], lambda q: q.dma_start(
                                out=okv[g][l, b4, kv, 0:m, :].rearrange("(a r) c -> a r c", a=16),
                                in_=ckv[g][l, b4, kv, 1:m + 1, :].rearrange("(a r) c -> a r c", a=16))))
                            shift_q.append(lambda g=g, l=l, m=m, n=n, b4=b4, kv=kv: s.dma('sp', 'shift', [], [], lambda q: q.dma_start(
                                out=okv[g][l, b4, kv, m:n - 1, :], in_=ckv[g][l, b4, kv, m + 1:n, :])))
            for l in range(2):
                shift_q.append(lambda l=l: s.dma('sp', 'shift', [], [], lambda q: q.dma_start(out=convs_o[l], in_=sconv[l, :, 1:30, :])))
        emit_shift_copies()
        s.barrier()

        if _STOP_AFTER == 'pre':
            s.barrier(final=True)
            return nc
        def norm_transpose(st, ph, xt, nb, g_sb_col, nT, tag, ps_pool):
            norm_part(st, xt, nb)
            trans_part(st, nb, g_sb_col, nT, ps_pool)

        def norm_part(st, xt, nb, part=None):
            ss, rstd, junk, xn = st['ss'], st['rstd'], st['junk'], st['xn']
            if part is not None:
                if part == 0:
                    for b in range(nb):
                        s.op('act', [xt], ['junk', 'ss'], lambda e, b=b: e.activation(
                            out=junk[:], in_=xt_ap(xt)[:, b, :], func=AF.Square, accum_out=ss[:, b:b + 1]))
                elif part == 1:
                    s.op('dve', ['ss'], ['rstd'], lambda e: e.tensor_scalar(
                        out=rstd[:, 0:nb], in0=ss[:, 0:nb], scalar1=1.0 / D, scalar2=RMS_EPS, op0=ALU.mult, op1=ALU.add))
                    s.op('act', ['rstd'], ['rstd'], lambda e: e.activation(out=rstd[:, 0:nb], in_=rstd[:, 0:nb], func=AF.Sqrt))
                    s.op('dve', ['rstd'], ['rstd'], lambda e: e.reciprocal(out=rstd[:, 0:nb], in_=rstd[:, 0:nb]))
                else:
                    for b in range(nb):
                        s.op('act', [xt, 'rstd'], ['xn'], lambda e, b=b: e.activation(
                            out=xn[:, b, :], in_=xt_ap(xt)[:, b, :], func=AF.Copy, scale=rstd[:, b:b + 1]))
                return
            for b in range(nb):
                s.op('act', [xt], ['junk', 'ss'], lambda e, b=b: e.activation(
                    out=junk[:], in_=xt_ap(xt)[:, b, :], func=AF.Square, accum_out=ss[:, b:b + 1]))
            s.op('dve', ['ss'], ['rstd'], lambda e: e.tensor_scalar(
                out=rstd[:, 0:nb], in0=ss[:, 0:nb], scalar1=1.0 / D, scalar2=RMS_EPS, op0=ALU.mult, op1=ALU.add))
            s.op('act', ['rstd'], ['rstd'], lambda e: e.activation(out=rstd[:, 0:nb], in_=rstd[:, 0:nb], func=AF.Sqrt))
            s.op('dve', ['rstd'], ['rstd'], lambda e: e.reciprocal(out=rstd[:, 0:nb], in_=rstd[:, 0:nb]))
            for b in range(nb):
                s.op('act', [xt, 'rstd'], ['xn'], lambda e, b=b: e.activation(
                    out=xn[:, b, :], in_=xt_ap(xt)[:, b, :], func=AF.Copy, scale=rstd[:, b:b + 1]))

        def trans_part(st, nb, g_sb_col, nT, ps_pool):
            xn = st['xn']
            for k in range(8):
                pT = ps_pool[k % 2]
                s.group('pe', ['xn', 'ident_b'], [pT], [
                    (lambda e, b=b, k=k, pT=pT: e.transpose(out=ps_ap(pT)[:, b, :], in_=xn[:, b, k * 128:(k + 1) * 128],
                                                            identity=ident_b[:])) for b in range(nb)])
                s.op('dve', [pT, g_sb_col[0]], [nT], lambda e, k=k, pT=pT: e.tensor_scalar(
                    out=nt_ap(nT)[:, k, 0:nb * 128].rearrange("p (b n) -> p b n", n=128), in0=ps_ap(pT)[:, 0:nb, :], scalar1=g_sb_col[1][:, k:k + 1],
                    scalar2=None, op0=ALU.mult))

        bufs = {}

        def xt_ap(name):
            return bufs[name]

        def ps_ap(name):
            return bufs[name]

        def nt_ap(name):
            return bufs[name]

        def load_weight(st, name, src, kchunks, ncols, q='pool'):
            wsb = st.enter_context(nc.sbuf_tensor(un(name), [128, kchunks, ncols], BF16))
            for k in range(kchunks):
                s.dma(q, 'w' + name, [], [name], lambda qq, k=k: qq.dma_start(out=wsb[:, k, :], in_=src[k * 128:(k + 1) * 128, :]), nowaw=(k > 0))
            return wsb

        def load_weight_cols(st, name, src, kchunks, ncols, blk, order=None):
            wsb = st.enter_context(nc.sbuf_tensor(un(name), [128, kchunks, ncols], BF16))
            nblk = ncols // blk
            for cb in (order if order is not None else range(nblk)):
                s.dma('pool', 'w%s%d' % (name, cb), [], ['%s%d' % (name, cb)], lambda qq, cb=cb: qq.dma_start(
                    out=wsb[:, :, cb * blk:(cb + 1) * blk],
                    in_=src[:, cb * blk:(cb + 1) * blk].rearrange("(k p) n -> p k n", p=128)))
            return wsb

        for l in range(2):
            lo = 0 if l == 0 else 2048
            flo = lo + 2048
            xin = xw if l == 0 else x1

            with ExitStack() as st:
              if dbg is None:
                  sb = lambda n, sh, dt: st.enter_context(nc.sbuf_tensor(un(n), sh, dt))
                  ps = lambda n, sh, dt: st.enter_context(nc.psum_tensor(un(n), sh, dt))
                  win = load_weight_cols(st, "win", w_in[l], 8, DIN, 768, order=[1, 2, 0, 3, 4])
                  stn = {'ss': sb("ss", [128, 4], F32), 'rstd': sb("rstd", [128, 4], F32),
                         'junk': sb("junk", [128, D], F32), 'xn': sb("xn", [128, 4, D], BF16)}
                  for i in range(2):
                      bufs['xtA%d' % i] = sb("xtA%d" % i, [128, 4, D], F32)
                      bufs['nTA%d' % i] = sb("nTA%d" % i, [128, 8, 512], BF16)
                      bufs['pTA%d' % i] = ps("pTA%d" % i, [128, 4, 128], BF16)
                      bufs['pz%d' % i] = ps("pz%d" % i, [128, 512], F32)
                      bufs['qkb%d' % i] = sb("qkb%d" % i, [128, 1536], BF16)
                      bufs['vaug%d' % i] = sb("vaug%d" % i, [128, 12, 65], BF16)
                  for i in range(4):
                      bufs['pg%d' % i] = ps("pg%d" % i, [128, 512], F32)
                  bufs['zt0'] = sb("zt0", [128, 2304], F32)
                  bufs['zt1'] = sb("zt1", [128, 2304], F32)
                  bufs['uT0'] = sb("uT0", [128, 6, 512], F32)
                  bufs['ub0'] = sb("ub0", [128, 6, 512], BF16)
                  sig = [sb("sig%d" % i, [128, 512], F32) for i in range(2)]
                  rt = [sb("rt%d" % i, [128, 24, 8], F32) for i in range(4)]
                  ones12 = sb("ones12", [128, 12, 1], F32)
                  s.op('dve', [], ['ones12'], lambda e: e.memset(ones12[:], 1.0))

                  tlA = tiles_for(lo)

                  def loadA(ti):
                      c0, nb = tlA[ti]
                      xt = 'xtA%d' % (ti % 2)
                      s.dma('sp', xt, [], [xt], lambda q: q.dma_start(
                          out=bufs[xt][:, 0:nb, :], in_=xin[c0:c0 + nb * 128, :].rearrange("(b p) d -> p b d", p=128)))
                  loadA(0)
                  norm_part(stn, 'xtA0', tlA[0][1])
                  trans_part(stn, tlA[0][1], ('gmix', gmix_sb[:, l, :]), 'nTA0', ['pTA0', 'pTA1'])
                  for ti, (c0, nb) in enumerate(tlA):
                      if ti + 1 < len(tlA):
                          loadA(ti + 1)
                      N = nb * 128
                      xt = 'xtA%d' % (ti % 2)
                      nT = 'nTA%d' % (ti % 2)
                      nTt = bufs[nT]
                      for b in range(nb):
                          if b >= 1 and ti + 1 < len(tlA):
                              norm_part(stn, 'xtA%d' % ((ti + 1) % 2), tlA[ti + 1][1], part=b - 1)
                          blk = c0 // 128 + b
                          bi = (ti * 4 + b) % 2
                          zt, qkb, vaug = 'zt%d' % bi, 'qkb%d' % bi, 'vaug%d' % bi
                          ztt, qkbt, vaugt = bufs[zt], bufs[qkb], bufs[vaug]
                          pieces = [(0, 512), (512, 1024), (1024, 1536), (1536, 2048), (2048, 2304)]
                          if c0 + 512 <= flo:
                              pieces = [(768, 1280), (1280, 1792), (1792, 2304)]
                          for pi, (a0, a1) in enumerate(pieces):
                              pz = 'pz%d' % (pi % 2)
                              s.group('pe', [nT] + ['win%d' % cb for cb in range(a0 // 768, (a1 - 1) // 768 + 1)], [pz], [
                                  (lambda e, k=k, pz=pz, a0=a0, a1=a1, b=b: e.matmul(
                                      out=bufs[pz][:, 0:a1 - a0], lhsT=nTt[:, k, b * 128:(b + 1) * 128],
                                      rhs=win[:, k, a0:a1], start=(k == 0), stop=(k == 7))) for k in range(8)])
                              s.op('act', [pz], [zt], lambda e, pz=pz, a0=a0, a1=a1, ztt=ztt: e.activation(
                                  out=ztt[:, a0:a1], in_=bufs[pz][:, 0:a1 - a0], func=AF.Copy))
                          zv = ztt[:, 0:1536].rearrange("p (h d) -> p h d", d=64)
                          x1v, x2v = zv[:, :, 0:8], zv[:, :, 8:16]
                          cb = cos_sb[:, blk, :]
                          sn = sin_sb[:, blk, :]
                          cbb = bass.AP(cb.tensor, cb.offset, [list(cb.ap[0]), [0, 24], list(cb.ap[-1])])
                          snb = bass.AP(sn.tensor, sn.offset, [list(sn.ap[0]), [0, 24], list(sn.ap[-1])])
                          s.op('dve', [zt], ['rt0'], lambda e, x1v=x1v, cbb=cbb: e.tensor_tensor(out=rt[0][:], in0=x1v, in1=cbb, op=ALU.mult))
                          s.op('dve', [zt], ['rt1'], lambda e, x2v=x2v, snb=snb: e.tensor_tensor(out=rt[1][:], in0=x2v, in1=snb, op=ALU.mult))
                          s.op('dve', [zt], ['rt2'], lambda e, x2v=x2v, cbb=cbb: e.tensor_tensor(out=rt[2][:], in0=x2v, in1=cbb, op=ALU.mult))
                          s.op('dve', [zt], ['rt3'], lambda e, x1v=x1v, snb=snb: e.tensor_tensor(out=rt[3][:], in0=x1v, in1=snb, op=ALU.mult))
                          s.op('dve', ['rt0', 'rt1'], [zt], lambda e, x1v=x1v: e.tensor_tensor(out=x1v, in0=rt[0][:], in1=rt[1][:], op=ALU.subtract))
                          s.op('dve', ['rt2', 'rt3'], [zt], lambda e, x2v=x2v: e.tensor_tensor(out=x2v, in0=rt[2][:], in1=rt[3][:], op=ALU.add))
                          s.op('pool', [zt], [qkb], lambda e, ztt=ztt, qkbt=qkbt: e.tensor_copy(out=qkbt[:], in_=ztt[:, 0:1536]))
                          s.op('pool', [zt], [vaug], lambda e, ztt=ztt, vaugt=vaugt: e.tensor_copy(
                              out=vaugt[:, :, 0:64], in_=ztt[:, 1536:2304].rearrange("p (h d) -> p h d", d=64)))
                          s.op('pool', ['ones12'], [vaug], lambda e, vaugt=vaugt, blk=blk: e.tensor_scalar(
                              out=vaugt[:, :, 64:65], in0=ones12[:], scalar1=valid_sb[:, blk:blk + 1], scalar2=None, op0=ALU.mult))
                          r0 = c0 + b * 128
                          s.dma('sp', qkb, [qkb], [], lambda q, qkbt=qkbt, r0=r0: q.dma_start(out=qs[r0:r0 + 128, :], in_=qkbt[:, 0:768]))
                          s.dma('sp', qkb, [qkb], [], lambda q, qkbt=qkbt, r0=r0: q.dma_start(out=ks[r0:r0 + 128, :], in_=qkbt[:, 768:1536]))
                          s.dma('sp', vaug, [vaug], [], lambda q, vaugt=vaugt, r0=r0: q.dma_start(
                              out=vs[r0:r0 + 128, :], in_=vaugt[:].rearrange("p h d -> p (h d)")))
                          if 6144 <= r0 < 8192:
                              s.dma('sp', zt, [zt], [], lambda q, ztt=ztt, r0=r0: q.dma_start(out=kout[l, r0 - 6144:r0 - 6144 + 128, :], in_=ztt[:, 768:1536]))
                              s.dma('sp', zt, [zt], [], lambda q, ztt=ztt, r0=r0: q.dma_start(out=vout[l, r0 - 6144:r0 - 6144 + 128, :], in_=ztt[:, 1536:2304]))
                          if r0 == NW:
                              for g in range(3):
                                  n = NBUF[g]
                                  for kv in range(2):
                                      cs0 = 768 + 768 * kv + 256 * g
                                      s.dma('sp', zt, [zt], [], lambda q, ztt=ztt, g=g, kv=kv, n=n, cs0=cs0: q.dma_start(
                                          out=okv[g][l, :, kv, n - 1, :], in_=ztt[0:4, cs0:cs0 + 256]))
                      if c0 + 512 <= flo - 512:
                          if ti + 1 < len(tlA):
                              trans_part(stn, tlA[ti + 1][1], ('gmix', gmix_sb[:, l, :]), 'nTA%d' % ((ti + 1) % 2), ['pTA0', 'pTA1'])
                          continue
                      ui = 0
                      uT, ub = 'uT%d' % ui, 'ub%d' % ui
                      uTt, ubt = bufs[uT], bufs[ub]
                      for cc in range(6):
                          pa, pgn = 'pg%d' % ((cc % 2) * 2), 'pg%d' % ((cc % 2) * 2 + 1)
                          sg = sig[cc % 2]
                          for (pn, coff) in ((pa, 2304 + cc * 128), (pgn, 2304 + 768 + cc * 128)):
                              s.group('pe', [nT, 'win%d' % (coff // 768)], [pn], [
                                  (lambda e, k=k, pn=pn, coff=coff: e.matmul(
                                      out=bufs[pn][:, 0:N], lhsT=win[:, k, coff:coff + 128], rhs=nTt[:, k, 0:N],
                                      start=(k == 0), stop=(k == 7))) for k in range(8)])
                          s.op('act', [pgn], ['sig%d' % (cc % 2)], lambda e, pgn=pgn, sg=sg: e.activation(
                              out=sg[:, 0:N], in_=bufs[pgn][:, 0:N], func=AF.Sigmoid))
                          s.op('dve', [pa, 'sig%d' % (cc % 2)], [uT], lambda e, pa=pa, sg=sg, cc=cc, uTt=uTt: e.tensor_tensor(
                              out=uTt[:, cc, 0:N], in0=bufs[pa][:, 0:N], in1=sg[:, 0:N], op=ALU.mult))
                      s.op('pool', [uT], [ub], lambda e, uTt=uTt, ubt=ubt: e.tensor_copy(out=ubt[:, :, 0:N], in_=uTt[:, :, 0:N]))
                      s.dma('sp', ub, [ub], [], lambda q, ubt=ubt, c0=c0: q.dma_start(
                          out=us_T[:, c0:c0 + N].rearrange("(cc p) n -> p cc n", p=128), in_=ubt[:, :, 0:N]))
                      if c0 == NW - 512:
                          s.dma('sp', uT, [uT], [], lambda q, uTt=uTt: q.dma_start(
                              out=convp_o[l].rearrange("(cc p) j -> p cc j", p=128), in_=uTt[:, :, 482:512]))
                      if c0 == NW:
                          s.dma('sp', uT, [uT], [], lambda q, uTt=uTt: q.dma_start(
                              out=convs_new[l].rearrange("(cc p) j -> p cc j", p=128), in_=uTt[:, :, 0:4]))
                      if ti + 1 < len(tlA):
                          trans_part(stn, tlA[ti + 1][1], ('gmix', gmix_sb[:, l, :]), 'nTA%d' % ((ti + 1) % 2), ['pTA0', 'pTA1'])
                  s.barrier()
            if _STOP_AFTER == (l, 'A'):
                s.barrier(final=True)
                return nc

            with ExitStack() as st:
                sb = lambda n, sh, dt: st.enter_context(nc.sbuf_tensor(un(n), sh, dt))
                ps = lambda n, sh, dt: st.enter_context(nc.psum_tensor(un(n), sh, dt))
                acc = sb("acc", [128, 4, 2048], F32)
                accS = sb("accS", [128, 4, 4], F32)
                ab = sb("ab", [64, 4, 512], BF16)
                rr = sb("rr", [128, 512], F32)
                for i in range(2):
                    bufs['Qt%d' % i] = sb("Qt%d" % i, [128, 256], BF16)
                    bufs['Kt%d' % i] = sb("Kt%d" % i, [128, 2, 256], BF16)
                    bufs['Vt%d' % i] = sb("Vt%d" % i, [128, 2, 4, 65], BF16)
                    bufs['TT%d' % i] = sb("TT%d" % i, [128, 8, 128], BF16)
                    bufs['Pt%d' % i] = sb("Pt%d" % i, [128, 1024], BF16)
                    bufs["pTT%d" % i] = ps("pTT%d" % i, [128, 8, 128], BF16)
                    bufs['pS%d' % i] = ps("pS%d" % i, [128, 1024], F32)
                    bufs['pO%d' % i] = ps("pO%d" % i, [128, 4, 128], F32)
                for i in range(2):
                    s.op('dve', [], ['Vt%d' % i], lambda e, i=i: e.memset(bufs['Vt%d' % i][:], 1.0))
                    s.op('dve', [], ['TT%d' % i], lambda e, i=i: e.memset(bufs['TT%d' % i][:], 0.0))
                sel_f = sb("sel_f", [128, 64], F32)
                s.op('dve', [], ['sel_f'], lambda e: e.memset(sel_f[:], 0.0))
                s.op('dve', ['sel_f'], ['sel_f'], lambda e: e.memset(sel_f[64:65, :], 1.0))
                s.op('dve', [], ['accS'], lambda e: e.memset(accS[:], 0.0))
                for i in range(2, 4):
                    bufs['Qt%d' % i] = sb("Qt%d" % i, [128, 256], BF16)
                    bufs['Kt%d' % i] = sb("Kt%d" % i, [128, 2, 256], BF16)
                    bufs['Vt%d' % i] = sb("Vt%d" % i, [128, 2, 4, 65], BF16)
                for i in range(4):
                    bufs['VtS%d' % i] = sb("VtS%d" % i, [128, 2, 4, 65], BF16)
                    s.op('dve', [], ['VtS%d' % i], lambda e, i=i: e.memset(bufs['VtS%d' % i][:], 1.0))
                units = []

                def unit_load(idx):
                    u = units[idx]
                    g = u['g']
                    Qt, Kt = 'Qt%d' % (idx % 4), 'Kt%d' % (idx % 4)
                    Vt = ('VtS%d' % (idx % 4)) if u['samp'] else ('Vt%d' % (idx % 4))
                    Qtt, Ktt, Vtt = bufs[Qt], bufs[Kt], bufs[Vt]
                    s.dma('sp', Qt, [], [Qt], lambda q: q.dma_start(out=Qtt[:], in_=u['q']))
                    if not u['samp']:
                        s.dma('sp', Kt, [], [Kt], lambda q: q.dma_start(
                            out=Ktt[:], in_=u['k'].rearrange("(two i) c -> i two c", two=2)))
                        s.dma('sp', Vt, [], [Vt], lambda q: q.dma_start(
                            out=Vtt[:], in_=u['v'].rearrange("(two i) (h d) -> i two h d", two=2, d=65)))
                    else:
                        s.dma('pool', Kt + 'p', [], [Kt], lambda q: q.dma_start(out=Ktt[:, 0, :], in_=u['kp']))
                        s.dma('sp', Kt, [], [Kt], lambda q: q.dma_start(out=Ktt[:, 1, :], in_=u['k']), nowaw=True)
                        s.dma('pool', Vt + 'p', [], [Vt], lambda q: q.dma_start(
                            out=Vtt[:, 0, :, 0:64], in_=u['vp'].rearrange("p (h d) -> p h d", d=64)))
                        s.dma('sp', Vt, [], [Vt], lambda q: q.dma_start(
                            out=Vtt[:, 1, :, :], in_=u['v'].rearrange("p (h d) -> p h d", d=65)), nowaw=True)

                def unit_compute(idx):
                    u = units[idx]
                    i = idx % 2
                    Qt, Kt = 'Qt%d' % (idx % 4), 'Kt%d' % (idx % 4)
                    Vt = ('VtS%d' % (idx % 4)) if u['samp'] else ('Vt%d' % (idx % 4))
                    TT, Pt = 'TT%d' % i, 'Pt%d' % i
                    pTT, pS, pO = 'pTT%d' % i, 'pS%d' % i, 'pO%d' % i
                    Qtt, Ktt, Vtt, TTt, Ptt = bufs[Qt], bufs[Kt], bufs[Vt], bufs[TT], bufs[Pt]
                    pTTt, pSt, pOt = bufs[pTT], bufs[pS], bufs[pO]
                    Ph = [Pt + 'h%d' % h for h in range(4)]
                    fns = []
                    for pr in range(2):
                        fns.append(lambda e, pr=pr: e.transpose(out=pTTt[:, pr, :], in_=Qtt[:, pr * 128:(pr + 1) * 128], identity=ident_b[:]))
                        for bk in range(2):
                            fns.append(lambda e, pr=pr, bk=bk: e.transpose(out=pTTt[:, 2 + pr * 2 + bk, :],
                                                                           in_=Ktt[:, bk, pr * 128:(pr + 1) * 128], identity=ident_b[:]))
                    s.group('pe', [Qt, Kt, 'ident_b'], [pTT], fns)
                    s.op('dve', [pTT], [TT], lambda e: e.tensor_copy(out=TTt[:, 0:4, :], in_=pTTt[:, 2:6, :]))
                    s.op('dve', [pTT], [TT], lambda e: e.tensor_copy(out=TTt[0:64, 4:7:2, :], in_=pTTt[0:64, 0:2, :]), nowaw=True)
                    s.op('dve', [pTT], [TT], lambda e: e.tensor_copy(out=TTt[64:128, 5:8:2, :], in_=pTTt[64:128, 0:2, :]), nowaw=True)
                    fns = []
                    for h in range(4):
                        pr = h // 2
                        for bk in range(2):
                            fns.append(lambda e, h=h, pr=pr, bk=bk: e.matmul(
                                out=pSt[:, h * 256 + bk * 128:h * 256 + (bk + 1) * 128], lhsT=TTt[:, pr * 2 + bk, :],
                                rhs=TTt[:, 4 + 2 * pr + (h % 2), :], start=True, stop=True))
                    s.group('pe', [TT], [pS], fns)
                    for hh in range(2):
                        s.op('act', [pS], [Ph[2 * hh], Ph[2 * hh + 1]], lambda e, hh=hh: e.activation(
                            out=Ptt[:, 512 * hh:512 * hh + 512], in_=pSt[:, 512 * hh:512 * hh + 512], func=AF.Exp, scale=0.125))
                    mk = mask_b[:]
                    mkb = bass.AP(mk.tensor, mk.offset, [list(mk.ap[0]), [0, 4], list(mk.ap[-1])])
                    s.op('dve', Ph + ['mask_b'], Ph, lambda e: e.tensor_tensor(
                        out=Ptt[:, :].rearrange("p (h c) -> p h c", h=4), in0=Ptt[:, :].rearrange("p (h c) -> p h c", h=4),
                        in1=mkb, op=ALU.mult))
                    fns = []
                    for h in range(4):
                        for bk in range(2):
                            fns.append(lambda e, h=h, bk=bk: e.matmul(
                                out=pOt[0:65, h, :], lhsT=Vtt[:, bk, h, :], rhs=Ptt[:, h * 256 + bk * 128:h * 256 + (bk + 1) * 128],
                                start=(bk == 0), stop=(bk == 1)))
                    u['_pv'] = (fns, Vt, Ph, pO, pOt)

                def unit_stage2(idx):
                    u = units[idx]
                    fns, Vt, Ph, pO, pOt = u['_pv']
                    s.group('pe', [Vt] + Ph, [pO], fns)
                    if u['acc_res'] == 'acc':
                        s.op('dve', [pO, 'acc'], ['acc'], lambda e: e.tensor_tensor(
                            out=u['acc_view'], in0=pOt[0:65, :, :], in1=u['acc_view'], op=ALU.add))
                    else:
                        s.op('dve', [pO, 'accS'], ['accS'], lambda e: e.tensor_tensor(
                            out=u['acc_view'], in0=pOt[0:65, :, 0:1], in1=u['acc_view'], op=ALU.add))

                def normalize(acc_t, acc_res, col0, ncols, dst, dst_res):
                    if ncols <= 128:
                        pBres, pBap = 'pO0', bufs['pO0'][0:64, 0, 0:ncols]
                    else:
                        pBres, pBap = 'pS0', bufs['pS0'][0:64, 0:ncols]
                    for h in range(4):
                        s.op('dve', [acc_res], ['rr'], lambda e, h=h: e.tensor_scalar(
                            out=rr[:, 0:ncols], in0=acc_t[:, h, col0:col0 + ncols], scalar1=1e-30, scalar2=None, op0=ALU.max))
                        s.op('dve', ['rr'], ['rr'], lambda e: e.reciprocal(out=rr[:, 0:ncols], in_=rr[:, 0:ncols]))
                        s.op('pe', ['rr', 'sel_f'], [pBres], lambda e: e.matmul(
                            out=pBap, lhsT=sel_f[:, :], rhs=rr[:, 0:ncols], start=True, stop=True))
                        s.op('dve', [acc_res, pBres], ['ab'], lambda e, h=h: e.tensor_tensor(
                            out=ab[:, h, 0:ncols], in0=acc_t[0:64, h, col0:col0 + ncols], in1=pBap, op=ALU.mult))
                    s.dma('sp', 'ab', ['ab'], dst_res, lambda q: q.dma_start(out=dst, in_=ab[:, :, 0:ncols]))

                items = []
                for S0 in (range(flo, NW, 2048) if dbg is None else [flo]):
                    items.append(('f', lambda: s.op('dve', [], ['acc'], lambda e: e.memset(acc[:], 0.0))))
                    for g in range(3):
                        d = DILS[g]
                        for n in range(16 // d):
                            for r in range(d):
                                base = S0 + 128 * d * n + r
                                off = 128 * d * n + r
                                rows_c = slice(base, base + 127 * d + 1, d)
                                rows_a = slice(base - 128 * d, base + 127 * d + 1, d)
                                units.append(dict(g=g, samp=False, q=qs[rows_c, 256 * g:256 * g + 256],
                                                  k=ks[rows_a, 256 * g:256 * g + 256], v=vs[rows_a, 260 * g:260 * g + 260],
                                                  acc_res='acc', acc_view=acc[0:65, :, off:off + 127 * d + 1:d]))
                                items.append(('u', len(units) - 1))
                    for c in range(4):
                        items.append(('f', lambda c=c, S0=S0: normalize(
                            acc, 'acc', c * 512, 512,
                            ac_T[0:256, S0 + c * 512:S0 + c * 512 + 512].rearrange("(h p) n -> p h n", p=64), [])))
                zz = sb("zz", [128, 8, 128], BF16)
                s.op('dve', [], ['zz'], lambda e: e.memset(zz[:], 0.0))
                s.dma('sp', 'zz', ['zz'], ['acT_s'], lambda q: q.dma_start(
                    out=ac_T[:, NW:NW + 128].rearrange("(k p) n -> p k n", p=128), in_=zz[:]))
                for b in (range(4) if not _NOSAMP else []):
                    for g in range(3):
                        d = DILS[g]
                        row = NW + b
                        units.append(dict(g=g, samp=True, q=bcast_rows(qs[row:row + 1, 256 * g:256 * g + 256], 128),
                                          kp=ckv[g][l, b, 0, 0:127 * d + 1:d, :], k=bcast_rows(ks[row:row + 1, 256 * g:256 * g + 256], 128),
                                          vp=ckv[g][l, b, 1, 0:127 * d + 1:d, :], v=bcast_rows(vs[row:row + 1, 260 * g:260 * g + 260], 128),
                                          acc_res='accS', acc_view=accS[0:65, :, b:b + 1]))
                        items.append(('u', len(units) - 1))
                items.append(('f', lambda: normalize(accS, 'accS', 0, 4, ac_T[0:256, NW:NW + 4].rearrange("(h p) n -> p h n", p=64), ['acT_s'])))
                nload = [0]
                pend = [None]
                for it in items:
                    if it[0] == 'u':
                        while nload[0] <= min(it[1] + _PF, len(units) - 1):
                            unit_load(nload[0])
                            nload[0] += 1
                        unit_compute(it[1])
                        if pend[0] is not None:
                            unit_stage2(pend[0])
                        pend[0] = it[1]
                    else:
                        if pend[0] is not None:
                            unit_stage2(pend[0])
                            pend[0] = None
                        it[1]()
                if pend[0] is not None:
                    unit_stage2(pend[0])
                s.barrier()
            if _STOP_AFTER == (l, 'B1'):
                s.barrier(final=True)
                return nc

            with ExitStack() as st:
                sb = lambda n, sh, dt: st.enter_context(nc.sbuf_tensor(un(n), sh, dt))
                ps = lambda n, sh, dt: st.enter_context(nc.psum_tensor(un(n), sh, dt))
                dg = sb("dg", [128, 6, 31, 128], BF16)
                cw = sb("cw", [128, 6, 31], F32)
                cp = sb("cp", [128, 3, 6], F32)
                s.dma('sp', 'cw', [], ['cw'], lambda q: q.dma_start(out=cw[:], in_=convwT[l]))
                s.dma('sp', 'cp', [], ['cp'], lambda q: q.dma_start(out=cp[:], in_=convp[l]))
                for cc in range(6):
                    for j in range(31):
                        if (cc * 31 + j) % 2 == 0:
                            s.op('dve', ['cw', 'ident_b'], ['dg'], lambda e, cc=cc, j=j: e.tensor_scalar(
                                out=dg[:, cc, j, :], in0=ident_b[:], scalar1=cw[:, cc, j:j + 1], scalar2=None, op0=ALU.mult), nowaw=True)
                        else:
                            s.op('act', ['cw', 'ident_b'], ['dg'], lambda e, cc=cc, j=j: e.activation(
                                out=dg[:, cc, j, :], in_=ident_b[:], func=AF.Copy, scale=cw[:, cc, j:j + 1]), nowaw=True)
                for i in range(2):
                    bufs['uc%d' % i] = sb("uc%d" % i, [128, 6, 544], BF16)
                    bufs['cvb%d' % i] = sb("cvb%d" % i, [128, 6, 512], BF16)
                    bufs['pC%d' % i] = ps("pC%d" % i, [128, 512], F32)
                cfs = [sb("cf%d" % i, [128, 6, 512], F32) for i in range(2)]
                cfi = [0]
                cfb = sb("cfb", [128, 6, 512], BF16)
                sq = sb("sq", [128, 6, 512], BF16)
                pM = ps("pM", [128, 512], F32)
                pQ = ps("pQ", [128, 512], F32)
                pTr = ps("pTr", [128, 128], F32)
                mean = sb("mean", [128, 512], F32)
                msq = sb("msq", [128, 512], F32)
                var = sb("var", [128, 512], F32)
                rs = sb("rs", [128, 512], F32)
                mb = sb("mb", [128, 512], F32)
                tt = sb("tt", [128, 512], F32)
                stt = sb("stt", [128, DQ], F32)
                s.op('dve', [], ['stt'], lambda e: e.memset(stt[:], 0.0))
                usn = sb("usn", [128, 6, 4], BF16)

                def conv_tile(uc, N, cvb):
                    uct, cvbt = bufs[uc], bufs[cvb]
                    cf = cfs[cfi[0] % 2]
                    cfr = 'cf%d' % (cfi[0] % 2)
                    cfi[0] += 1
                    for cc in range(6):
                        pC = 'pC%d' % (cc % 2)
                        s.group('pe', [uc, 'dg'], [pC], [
                            (lambda e, j=j, cc=cc, pC=pC: e.matmul(out=bufs[pC][:, 0:N], lhsT=dg[:, cc, j, :], rhs=uct[:, cc, j:j + N],
                                                                  start=(j == 0), stop=(j == 30))) for j in range(31)])
                        s.op('act', [pC, 'cp'], [cfr], lambda e, cc=cc, pC=pC: e.activation(
                            out=cf[:, cc, 0:N], in_=bufs[pC][:, 0:N], func=AF.Identity, bias=cp[:, 0, cc:cc + 1]))
                        s.op('pool', [cfr], ['cfb'], lambda e, cc=cc: e.tensor_copy(out=cfb[:, cc, 0:N], in_=cf[:, cc, 0:N]))
                        s.op('dve', [cfr], ['sq'], lambda e, cc=cc: e.tensor_tensor(
                            out=sq[:, cc, 0:N], in0=cf[:, cc, 0:N], in1=cf[:, cc, 0:N], op=ALU.mult))
                    s.group('pe', ['cfb', 'ones_b'], ['pM'], [
                        (lambda e, cc=cc: e.matmul(out=pM[:, 0:N], lhsT=ones_b[:], rhs=cfb[:, cc, 0:N], start=(cc == 0), stop=(cc == 5)))
                        for cc in range(6)])
                    s.group('pe', ['sq', 'ones_b'], ['pQ'], [
                        (lambda e, cc=cc: e.matmul(out=pQ[:, 0:N], lhsT=ones_b[:], rhs=sq[:, cc, 0:N], start=(cc == 0), stop=(cc == 5)))
                        for cc in range(6)])
                    s.op('dve', ['pM'], ['mean'], lambda e: e.tensor_scalar(out=mean[:, 0:N], in0=pM[:, 0:N], scalar1=1.0 / DQ, scalar2=None, op0=ALU.mult))
                    s.op('dve', ['mean'], ['msq'], lambda e: e.tensor_tensor(out=msq[:, 0:N], in0=mean[:, 0:N], in1=mean[:, 0:N], op=ALU.mult))
                    s.op('dve', ['pQ', 'msq'], ['var'], lambda e: e.scalar_tensor_tensor(
                        out=var[:, 0:N], in0=pQ[:, 0:N], scalar=1.0 / DQ, in1=msq[:, 0:N], op0=ALU.mult, op1=ALU.subtract))
                    s.op('dve', ['var'], ['var'], lambda e: e.tensor_scalar(out=var[:, 0:N], in0=var[:, 0:N], scalar1=LN_EPS, scalar2=None, op0=ALU.add))
                    s.op('act', ['var'], ['rs'], lambda e: e.activation(out=rs[:, 0:N], in_=var[:, 0:N], func=AF.Sqrt))
                    s.op('dve', ['rs'], ['rs'], lambda e: e.reciprocal(out=rs[:, 0:N], in_=rs[:, 0:N]))
                    s.op('dve', ['rs', 'mean'], ['mb'], lambda e: e.tensor_tensor(out=mb[:, 0:N], in0=mean[:, 0:N], in1=rs[:, 0:N], op=ALU.mult))
                    for cc in range(6):
                        s.op('dve', [cfr, 'rs'], ['tt'], lambda e, cc=cc: e.tensor_tensor(out=tt[:, 0:N], in0=cf[:, cc, 0:N], in1=rs[:, 0:N], op=ALU.mult))
                        s.op('dve', ['tt', 'mb'], ['tt'], lambda e: e.tensor_tensor(out=tt[:, 0:N], in0=tt[:, 0:N], in1=mb[:, 0:N], op=ALU.subtract))
                        s.op('act', ['tt', 'cp'], [cvb], lambda e, cc=cc: e.activation(
                            out=cvbt[:, cc, 0:N], in_=tt[:, 0:N], func=AF.Silu, scale=cp[:, 1, cc:cc + 1], bias=cp[:, 2, cc:cc + 1]))

                tlB = list(range(flo, NW, 512))

                def loadB(ti):
                    c0 = tlB[ti]
                    uc = 'uc%d' % (ti % 2)
                    s.dma('sp', uc, [], [uc], lambda q: q.dma_start(
                        out=bufs[uc][:, :, 0:542], in_=us_T[:, c0 - 30:c0 + 512].rearrange("(cc p) n -> p cc n", p=128)))
                loadB(0)
                for ti, c0 in enumerate(tlB):
                    if ti + 1 < len(tlB):
                        loadB(ti + 1)
                    uc, cvb = 'uc%d' % (ti % 2), 'cvb%d' % (ti % 2)
                    conv_tile(uc, 512, cvb)
                    s.dma('sp', cvb, [cvb], [], lambda q, cvb=cvb, c0=c0: q.dma_start(
                        out=ac_T[256:1024, c0:c0 + 512].rearrange("(cc p) n -> p cc n", p=128), in_=bufs[cvb][:, :, :]))
                uc, cvb = 'uc0', 'cvb0'
                s.op('dve', [], [uc], lambda e: e.memset(bufs[uc][:], 0.0))
                s.dma('sp', 'usn', [], ['usn'], lambda q: q.dma_start(
                    out=usn[:], in_=us_T[:, NW:NW + 4].rearrange("(cc p) n -> p cc n", p=128)))
                for b in range(4):
                    s.dma('sp', 'stt', [], ['stt'], lambda q, b=b: q.dma_start(out=stt[0:30, :], in_=sconv[l, b]))
                    for cc in range(6):
                        s.op('pe', ['stt', 'ident_f'], ['pTr'], lambda e, cc=cc: e.transpose(
                            out=pTr[:, :], in_=stt[:, cc * 128:(cc + 1) * 128], identity=ident_f[:]))
                        s.op('act', ['pTr'], [uc], lambda e, cc=cc, b=b: e.activation(
                            out=bufs[uc][:, cc, 32 * b:32 * b + 30], in_=pTr[:, 0:30], func=AF.Copy))
                s.op('dve', ['usn'], [uc], lambda e: e.tensor_copy(out=bufs[uc][:, :, 30:127:32], in_=usn[:, :, :]))
                conv_tile(uc, 128, cvb)
                cvs = sb("cvs", [128, 6, 4], BF16)
                s.op('dve', [cvb], ['cvs'], lambda e: e.tensor_copy(out=cvs[:], in_=bufs[cvb][:, :, 0:97:32]))
                s.dma('sp', 'cvs', ['cvs'], [], lambda q: q.dma_start(
                    out=ac_T[256:1024, NW:NW + 4].rearrange("(cc p) n -> p cc n", p=128), in_=cvs[:]))
                s.barrier()
            if _STOP_AFTER == (l, 'B2'):
                s.barrier(final=True)
                return nc

            with ExitStack() as st:
                sb = lambda n, sh, dt: st.enter_context(nc.sbuf_tensor(un(n), sh, dt))
                ps = lambda n, sh, dt: st.enter_context(nc.psum_tensor(un(n), sh, dt))
                wo = load_weight_cols(st, "wo", w_o[l], 8, D, 512)
                stn = {'ss': sb("ss", [128, 4], F32), 'rstd': sb("rstd", [128, 4], F32),
                       'junk': sb("junk", [128, D], F32), 'xn': sb("xn", [128, 4, D], BF16)}
                for i in range(2):
                    bufs['xtC%d' % i] = sb("xtC%d" % i, [128, 4, D], F32)
                    bufs['acC%d' % i] = sb("acC%d" % i, [128, 8, 512], BF16)
                    bufs['htC%d' % i] = sb("htC%d" % i, [128, 4, D], F32)
                    bufs['n2C%d' % i] = sb("n2C%d" % i, [128, 8, 512], BF16)
                    bufs['pTC%d' % i] = ps("pTC%d" % i, [128, 4, 128], BF16)
                for i in range(4):
                    bufs['pH%d' % i] = ps("pH%d" % i, [128, 512], F32)
                tlC = tiles_for(flo)

                def loadC1(ti):
                    c0, nb = tlC[ti]
                    xt, acn = 'xtC%d' % (ti % 2), 'acC%d' % (ti % 2)
                    s.dma('sp', acn, [], [acn], lambda q: q.dma_start(
                        out=bufs[acn][:, :, 0:nb * 128], in_=ac_T[:, c0:c0 + nb * 128].rearrange("(k p) n -> p k n", p=128)))
                    s.dma('sp', xt, [], [xt], lambda q: q.dma_start(
                        out=bufs[xt][:, 0:nb, :], in_=xin[c0:c0 + nb * 128, :].rearrange("(b p) d -> p b d", p=128)))
                def mmC1(ti):
                    c0, nb = tlC[ti]
                    xt, acn, ht = 'xtC%d' % (ti % 2), 'acC%d' % (ti % 2), 'htC%d' % (ti % 2)
                    xtt, act_, htt = bufs[xt], bufs[acn], bufs[ht]
                    for b in range(nb):
                        for hf in range(2):
                            pH = 'pH%d' % ((b * 2 + hf) % 4)
                            s.group('pe', [acn, 'wo%d' % hf], [pH], [
                                (lambda e, k=k, b=b, hf=hf, pH=pH: e.matmul(
                                    out=bufs[pH][:], lhsT=act_[:, k, b * 128:(b + 1) * 128], rhs=wo[:, k, hf * 512:(hf + 1) * 512],
                                    start=(k == 0), stop=(k == 7))) for k in range(8)])
                            s.op('dve', [pH, xt], [ht], lambda e, b=b, hf=hf, pH=pH: e.tensor_tensor(
                                out=htt[:, b, hf * 512:(hf + 1) * 512], in0=bufs[pH][:], in1=xtt[:, b, hf * 512:(hf + 1) * 512], op=ALU.add))

                def tailC1(ti):
                    c0, nb = tlC[ti]
                    N = nb * 128
                    ht, n2 = 'htC%d' % (ti % 2), 'n2C%d' % (ti % 2)
                    htt, n2t = bufs[ht], bufs[n2]
                    s.dma('sp', ht, [ht], [], lambda q: q.dma_start(
                        out=hs[c0:c0 + nb * 128, :].rearrange("(b p) d -> p b d", p=128), in_=htt[:, 0:nb, :]))
                    norm_transpose(stn, None, ht, nb, ('gffn', gffn_sb[:, l, :]), n2, 'C', ['pTC0', 'pTC1'])
                    s.dma('sp', n2, [n2], [], lambda q: q.dma_start(
                        out=n2_T[:, c0:c0 + N].rearrange("(k p) n -> p k n", p=128), in_=n2t[:, :, 0:N]))

                loadC1(0)
                if len(tlC) > 1:
                    loadC1(1)
                mmC1(0)
                for ti in range(len(tlC)):
                    if ti + 2 < len(tlC):
                        loadC1(ti + 2)
                    if ti + 1 < len(tlC):
                        mmC1(ti + 1)
                    tailC1(ti)
                s.barrier()
            if _STOP_AFTER == (l, 'C1'):
                s.barrier(final=True)
                return nc

            with ExitStack() as st:
                sb = lambda n, sh, dt: st.enter_context(nc.sbuf_tensor(un(n), sh, dt))
                ps = lambda n, sh, dt: st.enter_context(nc.psum_tensor(un(n), sh, dt))
                wu = load_weight_cols(st, "wu", w_up[l], 8, DFF, 1024)
                for i in range(2):
                    bufs['n2U%d' % i] = sb("n2U%d" % i, [128, 8, 512], BF16)
                    bufs['fU%d' % i] = sb("fU%d" % i, [128, 32, 512], BF16)
                    bufs['rl%d' % i] = sb("rl%d" % i, [128, 512], F32)
                for i in range(4):
                    bufs['pF%d' % i] = ps("pF%d" % i, [128, 512], F32)
                tlU = tiles_for(flo)

                def loadU(ti):
                    c0, nb = tlU[ti]
                    n2 = 'n2U%d' % (ti % 2)
                    s.dma('sp', n2, [], [n2], lambda q: q.dma_start(
                        out=bufs[n2][:, :, 0:nb * 128], in_=n2_T[:, c0:c0 + nb * 128].rearrange("(k p) n -> p k n", p=128)))
                loadU(0)
                for ti, (c0, nb) in enumerate(tlU):
                    if ti + 1 < len(tlU):
                        loadU(ti + 1)
                    N = nb * 128
                    n2, fU = 'n2U%d' % (ti % 2), 'fU%d' % (ti % 2)
                    n2t, fUt = bufs[n2], bufs[fU]
                    for m in range(32):
                        pF, rl = 'pF%d' % (m % 4), 'rl%d' % (m % 2)
                        s.group('pe', [n2, 'wu%d' % (m // 8)], [pF], [
                            (lambda e, k=k, m=m, pF=pF: e.matmul(out=bufs[pF][:, 0:N], lhsT=wu[:, k, m * 128:(m + 1) * 128],
                                                                rhs=n2t[:, k, 0:N], start=(k == 0), stop=(k == 7))) for k in range(8)])
                        s.op('act', [pF], [rl], lambda e, pF=pF, rl=rl: e.activation(out=bufs[rl][:, 0:N], in_=bufs[pF][:, 0:N], func=AF.Relu))
                        s.op('dve', [rl], [fU], lambda e, m=m, rl=rl: e.tensor_tensor(
                            out=fUt[:, m, 0:N], in0=bufs[rl][:, 0:N], in1=bufs[rl][:, 0:N], op=ALU.mult), nowaw=(m > 0))
                    s.dma('sp', fU, [fU], [], lambda q, fUt=fUt, c0=c0, N=N: q.dma_start(
                        out=f_T[:, c0:c0 + N].rearrange("(m p) n -> p m n", p=128), in_=fUt[:, :, 0:N]))
                    if l == 0:
                        emit_shift_some(9 if ti + 1 < len(tlU) else 1000)
                s.barrier()
            if _STOP_AFTER == (l, 'C2a'):
                s.barrier(final=True)
                return nc

            with ExitStack() as st:
                sb = lambda n, sh, dt: st.enter_context(nc.sbuf_tensor(un(n), sh, dt))
                ps = lambda n, sh, dt: st.enter_context(nc.psum_tensor(un(n), sh, dt))
                wd = load_weight_cols(st, "wd", w_down[l], 32, D, 512)
                ss2 = sb("ss2", [128, 4], F32)
                rstd2 = sb("rstd2", [128, 4], F32)
                junk2 = sb("junk2", [128, D], F32)
                for i in range(2):
                    bufs['fD%d' % i] = sb("fD%d" % i, [128, 32, 512], BF16)
                    bufs['hD%d' % i] = sb("hD%d" % i, [128, 4, D], F32)
                for i in range(4):
                    bufs['pY%d' % i] = ps("pY%d" % i, [128, 512], F32)
                tlD = tiles_for(flo)

                def loadD(ti):
                    c0, nb = tlD[ti]
                    fD, hD = 'fD%d' % (ti % 2), 'hD%d' % (ti % 2)
                    s.dma('sp', fD, [], [fD], lambda q: q.dma_start(
                        out=bufs[fD][:, :, 0:nb * 128], in_=f_T[:, c0:c0 + nb * 128].rearrange("(m p) n -> p m n", p=128)))
                    s.dma('sp', hD, [], [hD], lambda q: q.dma_start(
                        out=bufs[hD][:, 0:nb, :], in_=hs[c0:c0 + nb * 128, :].rearrange("(b p) d -> p b d", p=128)))
                loadD(0)
                for ti, (c0, nb) in enumerate(tlD):
                    if ti + 1 < len(tlD):
                        loadD(ti + 1)
                    N = nb * 128
                    fD, hD = 'fD%d' % (ti % 2), 'hD%d' % (ti % 2)
                    yD = hD
                    fDt, hDt = bufs[fD], bufs[hD]
                    yDt = hDt
                    for b in range(nb):
                        for hf in range(2):
                            pY = 'pY%d' % ((b * 2 + hf) % 4)
                            s.group('pe', [fD, 'wd%d' % hf], [pY], [
                                (lambda e, m=m, b=b, hf=hf, pY=pY: e.matmul(
                                    out=bufs[pY][:], lhsT=fDt[:, m, b * 128:(b + 1) * 128], rhs=wd[:, m, hf * 512:(hf + 1) * 512],
                                    start=(m == 0), stop=(m == 31))) for m in range(32)])
                            s.op('dve', [pY, hD], [yD], lambda e, b=b, hf=hf, pY=pY: e.tensor_tensor(
                                out=yDt[:, b, hf * 512:(hf + 1) * 512], in0=bufs[pY][:], in1=hDt[:, b, hf * 512:(hf + 1) * 512], op=ALU.add))
                    if l == 0:
                        for b in range(nb):
                            blk = c0 // 128 + b
                            s.op('dve', [yD, 'valid'], [yD], lambda e, b=b, blk=blk: e.tensor_scalar(
                                out=yDt[:, b, :], in0=yDt[:, b, :], scalar1=valid_sb[:, blk:blk + 1], scalar2=None, op0=ALU.mult))
                        s.dma('sp', yD, [yD], [], lambda q, yDt=yDt, c0=c0, nb=nb: q.dma_start(
                            out=x1[c0:c0 + nb * 128, :].rearrange("(b p) d -> p b d", p=128), in_=yDt[:, 0:nb, :]))
                    else:
                        for b in range(nb):
                            s.op('act', [yD], ['junk2', 'ss2'], lambda e, b=b: e.activation(
                                out=junk2[:], in_=yDt[:, b, :], func=AF.Square, accum_out=ss2[:, b:b + 1]))
                        s.op('dve', ['ss2'], ['rstd2'], lambda e: e.tensor_scalar(
                            out=rstd2[:, 0:nb], in0=ss2[:, 0:nb], scalar1=1.0 / D, scalar2=RMS_EPS, op0=ALU.mult, op1=ALU.add))
                        s.op('act', ['rstd2'], ['rstd2'], lambda e: e.activation(out=rstd2[:, 0:nb], in_=rstd2[:, 0:nb], func=AF.Sqrt))
                        s.op('dve', ['rstd2'], ['rstd2'], lambda e: e.reciprocal(out=rstd2[:, 0:nb], in_=rstd2[:, 0:nb]))
                        for b in range(nb):
                            s.op('dve', [yD, 'rstd2', 'gfin'], [yD], lambda e, b=b: e.scalar_tensor_tensor(
                                out=yDt[:, b, :], in0=yDt[:, b, :], scalar=rstd2[:, b:b + 1], in1=gfin_sb[:], op0=ALU.mult, op1=ALU.mult))
                        o0 = c0 - 4096
                        s.dma('sp', yD, [yD], [], lambda q, yDt=yDt, o0=o0, nb=nb: q.dma_start(
                            out=y_o[o0:o0 + nb * 128, :].rearrange("(b p) d -> p b d", p=128), in_=yDt[:, 0:nb, :]))
                s.barrier()
            if _STOP_AFTER == (l, 'C2b'):
                s.barrier(final=True)
                return nc
        s.barrier(final=True)
    return nc


def _rope_tables(pos):
    inv = (1.0 / (np.float32(500000.0) ** (np.arange(0, 16, 2, dtype=np.float32) / np.float32(16)))).astype(np.float32)
    ang = (pos.astype(np.float32)[:, None] * inv[None, :]).astype(np.float32)
    return np.cos(ang).astype(np.float32), np.sin(ang).astype(np.float32)


def kernel(x_prompt, x_sample, cache_kv_w128, cache_kv_w512, cache_kv_w2048, state_conv,
           w_in, w_o, conv_w, conv_b, conv_ln_g, conv_ln_b, norm_mix, norm_ffn, w_up, w_down, norm_final):
    f32 = np.float32
    x_prompt = np.asarray(x_prompt, f32)
    x_sample = np.asarray(x_sample, f32)
    caches = [np.asarray(c, f32) for c in (cache_kv_w128, cache_kv_w512, cache_kv_w2048)]
    state_conv = np.asarray(state_conv, f32)
    w_in = np.ascontiguousarray(np.asarray(w_in, f32))
    w_o = np.ascontiguousarray(np.asarray(w_o, f32))
    w_up = np.ascontiguousarray(np.asarray(w_up, f32))
    w_down = np.ascontiguousarray(np.asarray(w_down, f32))
    conv_w = np.asarray(conv_w, f32)
    nblk = NT // 128

    def pcc(v):
        return np.ascontiguousarray(np.asarray(v, f32).reshape(6, 128).T)

    convwT = np.stack([np.ascontiguousarray(conv_w[l].T.reshape(6, 128, 31).transpose(1, 0, 2)) for l in range(2)])
    convp = np.stack([np.stack([pcc(conv_b[l]), pcc(conv_ln_g[l]), pcc(conv_ln_b[l])], axis=1) for l in range(2)])
    gmix = np.stack([np.ascontiguousarray(np.asarray(norm_mix, f32)[l].reshape(8, 128).T) for l in range(2)])
    gffn = np.stack([np.ascontiguousarray(np.asarray(norm_ffn, f32)[l].reshape(8, 128).T) for l in range(2)])
    gfin = np.asarray(norm_final, f32).reshape(1, D)
    ident = np.eye(128, dtype=f32)
    kk = np.arange(128)[:, None]
    qq = np.arange(128)[None, :]
    mask = np.concatenate([(qq <= kk), (kk <= qq)], axis=1).astype(f32)

    in_maps = []
    for c in range(8):
        sq_, half = c // 2, c % 2
        w0 = -4096 if half == 0 else 0
        xwin = np.zeros((NT, D), f32)
        pos = np.zeros((NT,), np.int64)
        valid = np.zeros((NT,), f32)
        p = w0 + np.arange(NW)
        real = p >= 0
        xwin[:NW][real] = x_prompt[sq_, p[real]]
        pos[:NW][real] = p[real]
        valid[:NW][real] = 1.0
        xwin[NW:NW + 4] = x_sample[4 * c:4 * c + 4, 0]
        pos[NW:] = 16384
        valid[NW:NW + 4] = 1.0
        cos, sin = _rope_tables(pos)
        to_pb = lambda a: np.ascontiguousarray(a.reshape(nblk, 128, *a.shape[1:]).swapaxes(0, 1))
        m = {
            "xw": xwin, "validT": to_pb(valid), "cosT": to_pb(cos), "sinT": to_pb(sin),
            "ident": ident, "mask": mask, "w_in": w_in, "w_o": w_o, "w_up": w_up, "w_down": w_down,
            "convwT": convwT, "convp": convp, "gmix": gmix, "gffn": gffn, "gfin": gfin,
            "sconv": np.ascontiguousarray(state_conv[:, 4 * c:4 * c + 4]),
        }
        for g in range(3):
            m["ckv%d" % g] = np.ascontiguousarray(caches[g][:, 4 * c:4 * c + 4].reshape(2, 4, 2, NBUF[g], 256))
        in_maps.append(m)

    nc = build_program()
    res = run_bass_kernel_spmd(nc, in_maps, core_ids=list(range(8)))
    R = res.results

    y_prompt = np.zeros((4, 8192, D), f32)
    y_sample = np.zeros((32, 1, D), f32)
    kvp = [np.zeros((2, 4, 2, NBUF[g], 4, 64), f32) for g in range(3)]
    convp_out = np.zeros((2, 4, 30, DQ), f32)
    kvs = [np.zeros((2, 32, 2, NBUF[g], 4, 64), f32) for g in range(3)]
    convs_out = np.zeros((2, 32, 30, DQ), f32)
    for c in range(8):
        sq_, half = c // 2, c % 2
        r = R[c]
        y_prompt[sq_, half * 4096:(half + 1) * 4096] = r["y"][0:4096]
        y_sample[4 * c:4 * c + 4, 0] = r["y"][4096:4100]
        if half == 1:
            for g in range(3):
                n = NBUF[g]
                kvp[g][:, sq_, 0] = r["kout"][:, 2048 - n:, 256 * g:256 * g + 256].reshape(2, n, 4, 64)
                kvp[g][:, sq_, 1] = r["vout"][:, 2048 - n:, 256 * g:256 * g + 256].reshape(2, n, 4, 64)
            convp_out[:, sq_] = r["convp_o"].transpose(0, 2, 1)
        for g in range(3):
            kvs[g][:, 4 * c:4 * c + 4] = r["okv%d" % g].reshape(2, 4, 2, NBUF[g], 4, 64)
        convs_out[:, 4 * c:4 * c + 4, 0:29] = r["convs_o"]
        convs_out[:, 4 * c:4 * c + 4, 29] = r["convs_new"].transpose(0, 2, 1)
    return (y_prompt, y_sample, kvp[0], kvp[1], kvp[2], convp_out, kvs[0], kvs[1], kvs[2], convs_out)
```
